# Optimizing a Trainium2 kernel written in Bass

```python
import jax, jax.numpy as jnp
from jax import lax
import numpy as np

D_MODEL = 1024
BATCH = 32
SEQ = 2048
DEPTH = 1

CHUNK = 64
RET_HEADS = 4
RET_DK = 128
RET_DV = 256
RET_QK_WIDTH = RET_HEADS * RET_DK
RET_V_WIDTH = RET_HEADS * RET_DV
LRU_WIDTH = 1024
LRU_BLOCKS = 4
LRU_BLOCK = LRU_WIDTH // LRU_BLOCKS
LRU_CONV = 4
LRU_C = 8.0
LRU_MIN_RAD = 0.9
LRU_MAX_RAD = 0.999
D_FF = 3 * D_MODEL
FFN_CONV = 3
ROPE_BASE = 10000.0
RMS_EPS = 1e-6
GN_EPS = 1e-6

IN_SIZES = (RET_QK_WIDTH, RET_QK_WIDTH, RET_V_WIDTH, RET_V_WIDTH,
            LRU_WIDTH, LRU_WIDTH, D_MODEL, D_MODEL)
D_IN = sum(IN_SIZES)
SPLIT_POINTS = tuple(sum(IN_SIZES[:i + 1]) for i in range(len(IN_SIZES) - 1))

kernel_name = "chunk_causal_retention_rglru_gated_hybrid"


def rms_norm(x, w):
    x32 = x.astype(jnp.float32)
    y = x32 * lax.rsqrt(jnp.mean(x32 * x32, axis=-1, keepdims=True) + RMS_EPS)
    return (y * w.astype(jnp.float32)).astype(x.dtype)


def causal_depthwise_conv(x, w, b):
    k_width, channels = w.shape
    y = lax.conv_general_dilated(
        x, w[:, None, :].astype(x.dtype), window_strides=(1,),
        padding=[(k_width - 1, 0)], dimension_numbers=("NWC", "WIO", "NWC"),
        feature_group_count=channels)
    return y + b.astype(x.dtype)


def rotary(x, positions):
    half = x.shape[-1] // 2
    inv_freq = ROPE_BASE ** (-jnp.arange(half, dtype=jnp.float32) / half)
    ang = positions.astype(jnp.float32)[..., None] * inv_freq
    cos = jnp.cos(ang)[:, :, None, :]
    sin = jnp.sin(ang)[:, :, None, :]
    x32 = x.astype(jnp.float32)
    x1, x2 = x32[..., :half], x32[..., half:]
    return jnp.concatenate([x1 * cos - x2 * sin, x1 * sin + x2 * cos], axis=-1).astype(x.dtype)


def chunkwise_retention(q, k, v):
    bsz, seq, heads, dk = q.shape
    dv = v.shape[-1]
    n_chunks = seq // CHUNK
    log_gamma = jnp.log1p(-jnp.power(2.0, -5.0 - jnp.arange(heads, dtype=jnp.float32)))
    idx = jnp.arange(CHUNK, dtype=jnp.float32)
    dist = jnp.abs(idx[:, None] - idx[None, :])
    intra_decay = jnp.exp(log_gamma[:, None, None] * dist)
    q_decay = jnp.exp(log_gamma[:, None] * (idx + 1.0))
    k_decay = jnp.exp(log_gamma[:, None] * (CHUNK - 1.0 - idx))
    chunk_decay = jnp.exp(log_gamma * CHUNK)

    def to_chunks(t):
        return t.astype(jnp.float32).reshape(bsz, n_chunks, CHUNK, heads, -1).transpose(1, 0, 3, 2, 4)

    qc, kc, vc = to_chunks(q), to_chunks(k), to_chunks(v)

    def step(state, inp):
        qi, ki, vi = inp
        scores = jnp.einsum("bhid,bhjd->bhij", qi, ki) * intra_decay
        out = (jnp.einsum("bhij,bhjv->bhiv", scores, vi)
               + jnp.einsum("bhid,bhdv->bhiv", qi * q_decay[:, :, None], state))
        state = (state * chunk_decay[:, None, None]
                 + jnp.einsum("bhjd,bhjv->bhdv", ki * k_decay[:, :, None], vi))
        return state, out

    state0 = jnp.zeros((bsz, heads, dk, dv), jnp.float32)
    _, out = lax.scan(step, state0, (qc, kc, vc))
    return out.transpose(1, 0, 3, 2, 4).reshape(bsz, seq, heads, dv)


def head_group_norm(o, w):
    bsz, seq, heads, dv = o.shape
    mu = jnp.mean(o, axis=-1, keepdims=True)
    var = jnp.mean(jnp.square(o - mu), axis=-1, keepdims=True)
    y = (o - mu) * lax.rsqrt(var + GN_EPS)
    return y.reshape(bsz, seq, heads * dv) * w.astype(jnp.float32)


def _linear_recurrence_combine(e1, e2):
    a1, b1 = e1
    a2, b2 = e2
    return a1 * a2, a2 * b1 + b2


def rg_lru(x, w_r, b_r, w_i, b_i, lam):
    bsz, seq, width = x.shape
    x32 = x.astype(jnp.float32)
    xb = x32.reshape(bsz, seq, LRU_BLOCKS, LRU_BLOCK)
    r = jax.nn.sigmoid(jnp.einsum("bsni,nij->bsnj", xb, w_r.astype(jnp.float32))
                       + b_r.astype(jnp.float32)).reshape(bsz, seq, width)
    i = jax.nn.sigmoid(jnp.einsum("bsni,nij->bsnj", xb, w_i.astype(jnp.float32))
                       + b_i.astype(jnp.float32)).reshape(bsz, seq, width)
    log_a = -LRU_C * r * jax.nn.softplus(-lam.astype(jnp.float32))
    a = jnp.exp(log_a)
    b = jnp.sqrt(-jnp.expm1(2.0 * log_a)) * (i * x32)
    _, h = lax.associative_scan(_linear_recurrence_combine, (a, b), axis=1)
    return h


def setup_inputs(seed: int = 0) -> dict:
    key = jax.random.key(seed)
    ks = jax.random.split(key, 24)

    def nrm(k, shape, scale):
        return jax.random.normal(k, shape, jnp.float32) * scale

    def gain(k, shape):
        return 1.0 + 0.02 * jax.random.normal(k, shape, jnp.float32)

    x = jax.random.normal(ks[0], (BATCH, SEQ, D_MODEL), jnp.float32)
    start = jax.random.randint(ks[1], (BATCH, 1), 0, 64) * CHUNK
    positions = (start + jnp.arange(SEQ)[None, :]).astype(jnp.int32)

    u = jax.random.uniform(ks[12], (DEPTH, LRU_WIDTH), jnp.float32,
                           LRU_MIN_RAD ** 2, LRU_MAX_RAD ** 2)
    a0 = jnp.sqrt(u)
    lru_lambda = jnp.log(a0) - jnp.log1p(-a0)

    return {
        "x": x,
        "positions": positions,
        "norm1_w": gain(ks[2], (DEPTH, D_MODEL)),
        "w_in": nrm(ks[3], (DEPTH, D_MODEL, D_IN), D_MODEL ** -0.5),
        "merge_gate_b": nrm(ks[4], (DEPTH, 2, D_MODEL), 0.02),
        "ret_gn_w": gain(ks[5], (DEPTH, RET_V_WIDTH)),
        "w_ret_o": nrm(ks[6], (DEPTH, RET_V_WIDTH, D_MODEL), RET_V_WIDTH ** -0.5),
        "lru_conv_w": nrm(ks[7], (DEPTH, LRU_CONV, LRU_WIDTH), LRU_CONV ** -0.5),
        "lru_conv_b": nrm(ks[8], (DEPTH, LRU_WIDTH), 0.02),
        "lru_w_r": nrm(ks[9], (DEPTH, LRU_BLOCKS, LRU_BLOCK, LRU_BLOCK), LRU_BLOCK ** -0.5),
        "lru_b_r": nrm(ks[10], (DEPTH, LRU_BLOCKS, LRU_BLOCK), 0.02),
        "lru_w_i": nrm(ks[11], (DEPTH, LRU_BLOCKS, LRU_BLOCK, LRU_BLOCK), LRU_BLOCK ** -0.5),
        "lru_b_i": nrm(ks[13], (DEPTH, LRU_BLOCKS, LRU_BLOCK), 0.02),
        "lru_lambda": lru_lambda,
        "w_lru_o": nrm(ks[14], (DEPTH, LRU_WIDTH, D_MODEL), LRU_WIDTH ** -0.5),
        "w_out": nrm(ks[15], (DEPTH, D_MODEL, D_MODEL), D_MODEL ** -0.5),
        "norm2_w": gain(ks[16], (DEPTH, D_MODEL)),
        "ffn_w_up": nrm(ks[17], (DEPTH, D_MODEL, 2 * D_FF), D_MODEL ** -0.5),
        "ffn_conv_w": nrm(ks[18], (DEPTH, FFN_CONV, D_FF), FFN_CONV ** -0.5),
        "ffn_conv_b": nrm(ks[19], (DEPTH, D_FF), 0.02),
        "ffn_w_down": nrm(ks[20], (DEPTH, D_FF, D_MODEL), D_FF ** -0.5),
        "norm_f_w": gain(ks[21], (D_MODEL,)),
    }


def reference(x, positions, norm1_w, w_in, merge_gate_b, ret_gn_w, w_ret_o,
              lru_conv_w, lru_conv_b, lru_w_r, lru_b_r, lru_w_i, lru_b_i, lru_lambda,
              w_lru_o, w_out, norm2_w, ffn_w_up, ffn_conv_w, ffn_conv_b, ffn_w_down,
              norm_f_w):
    bsz, seq, _ = x.shape
    for l in range(DEPTH):
        h = rms_norm(x, norm1_w[l])
        proj = h @ w_in[l]
        q, k, v, g_ret, x_lru, y_lru, gate_ret, gate_lru = jnp.split(proj, SPLIT_POINTS, axis=-1)

        q = rotary(q.reshape(bsz, seq, RET_HEADS, RET_DK), positions)
        k = rotary(k.reshape(bsz, seq, RET_HEADS, RET_DK), positions) * (RET_DK ** -0.5)
        o = chunkwise_retention(q, k, v.reshape(bsz, seq, RET_HEADS, RET_DV))
        o = head_group_norm(o, ret_gn_w[l])
        y_a = (o * jax.nn.silu(g_ret.astype(jnp.float32))).astype(x.dtype) @ w_ret_o[l]

        xc = causal_depthwise_conv(x_lru, lru_conv_w[l], lru_conv_b[l])
        hl = rg_lru(xc, lru_w_r[l], lru_b_r[l], lru_w_i[l], lru_b_i[l], lru_lambda[l])
        y_b = (hl * jax.nn.gelu(y_lru.astype(jnp.float32))).astype(x.dtype) @ w_lru_o[l]

        mix = (jax.nn.sigmoid(gate_ret + merge_gate_b[l, 0]) * y_a
               + jax.nn.sigmoid(gate_lru + merge_gate_b[l, 1]) * y_b)
        x = x + mix @ w_out[l]

        h = rms_norm(x, norm2_w[l])
        up = h @ ffn_w_up[l]
        gate, val = jnp.split(up, [D_FF], axis=-1)
        gate = causal_depthwise_conv(gate, ffn_conv_w[l], ffn_conv_b[l])
        x = x + (jax.nn.gelu(gate) * val) @ ffn_w_down[l]
    return rms_norm(x, norm_f_w)
```

```python
import numpy as np
import ml_dtypes
from contextlib import ExitStack
import concourse.bass as bass
import concourse.mybir as mybir
from concourse.bass_utils import run_bass_kernel_spmd

F32 = mybir.dt.float32
BF16 = mybir.dt.bfloat16
I32 = mybir.dt.int32
AF = mybir.ActivationFunctionType
ALU = mybir.AluOpType
_DT_SIZE = {F32: 4, BF16: 2, I32: 4}
ENGS = ("pe", "act", "dve", "pool", "sp")


def ap_region(ap):
    es = _DT_SIZE[ap.dtype]
    pairs = ap.ap
    pstep = pairs[0][0]
    off = ap.offset
    lo = off % pstep if pstep > 0 else off
    span = 0
    for st, cnt in pairs[1:]:
        span += abs(st) * (cnt - 1)
    lo_b, hi_b = lo * es, (lo + span + 1) * es
    if ap.tensor.name in ("PS", "PTR"):
        lo_b = (lo_b // 2048) * 2048
        hi_b = ((hi_b + 2047) // 2048) * 2048
    return (ap.tensor.name, lo_b, hi_b)


class Op:
    __slots__ = ("eng", "fn", "seq", "waits", "token", "dma_sem")

    def __init__(self, eng, fn):
        self.eng = eng
        self.fn = fn
        self.waits = {}
        self.token = None
        self.dma_sem = None


class Sched:
    def __init__(self, nc):
        self.nc = nc
        self.ops = {e: [] for e in ENGS}
        self.track = {}
        self.dma_cnt = {}

    def _segs(self, name, lo, hi):
        lst = self.track.get(name)
        if lst is None:
            lst = []
        out = []
        new = []
        cur = lo
        lst.sort(key=lambda r: r[0])
        for r in lst:
            if r[1] <= lo or r[0] >= hi:
                new.append(r)
                continue
            if r[0] < lo:
                new.append([r[0], lo, r[2], dict(r[3])])
                r = [lo, r[1], r[2], r[3]]
            if r[1] > hi:
                new.append([hi, r[1], r[2], dict(r[3])])
                r = [r[0], hi, r[2], r[3]]
            if r[0] > cur:
                g = [cur, r[0], None, {}]
                new.append(g)
                out.append(g)
            new.append(r)
            out.append(r)
            cur = r[1]
        if cur < hi:
            g = [cur, hi, None, {}]
            new.append(g)
            out.append(g)
        self.track[name] = new
        return out

    @staticmethod
    def _add_wait(op, tok):
        if tok is None:
            return
        s, v = tok
        if op.waits.get(s, 0) < v:
            op.waits[s] = v

    def op(self, eng, fn, reads=(), writes=(), dma_slot=None):
        o = Op(eng, fn)
        lst = self.ops[eng]
        o.seq = len(lst) + 1
        if dma_slot is not None:
            c = self.dma_cnt.get(dma_slot, 0) + 16
            self.dma_cnt[dma_slot] = c
            o.dma_sem = dma_slot
            o.token = ("dma:" + dma_slot, c)
        else:
            o.token = (eng, o.seq)
        s, v = o.token
        rregs = [r if isinstance(r, tuple) else ap_region(r) for r in reads]
        wregs = [w if isinstance(w, tuple) else ap_region(w) for w in writes]
        for reg in rregs:
            for seg in self._segs(*reg):
                self._add_wait(o, seg[2])
        for reg in wregs:
            for seg in self._segs(*reg):
                self._add_wait(o, seg[2])
                for s2, v2 in seg[3].items():
                    self._add_wait(o, (s2, v2))
        for reg in rregs:
            for seg in self._segs(*reg):
                if seg[3].get(s, 0) < v:
                    seg[3][s] = v
        for reg in wregs:
            for seg in self._segs(*reg):
                seg[2] = o.token
                seg[3] = {}
        lst.append(o)
        return o

    def emit(self, final_waits_eng="sp", same_engine_sync=("act", "dve", "pool")):
        nc = self.nc
        need = {e: set() for e in ENGS}
        for e in ENGS:
            for o in self.ops[e]:
                for s, v in list(o.waits.items()):
                    if s.startswith("dma:"):
                        if s == "dma:" + str(o.dma_sem) and v >= o.token[1]:
                            del o.waits[s]
                        continue
                    if s == e and (e not in same_engine_sync or v >= o.seq):
                        del o.waits[s]
                        continue
                    need[s].add(v)
        rank = {}
        for e in ENGS:
            for i, v in enumerate(sorted(need[e])):
                rank[(e, v)] = i + 1
        stack = ExitStack()
        sems = {}
        for e in ENGS:
            sems[e] = stack.enter_context(nc.semaphore("s_" + e))
        for slot in self.dma_cnt:
            sems["dma:" + slot] = stack.enter_context(nc.semaphore("d_" + slot))
        sched = self

        def run_engine(ename, handle):
            waited = {}
            for o in sched.ops[ename]:
                for s, v in o.waits.items():
                    val = v if s.startswith("dma:") else rank[(s, v)]
                    if waited.get(s, 0) >= val:
                        continue
                    handle.wait_ge(sems[s], val)
                    waited[s] = val
                ins = o.fn(handle)
                if o.dma_sem is not None:
                    ins.then_inc(sems["dma:" + o.dma_sem], 16)
                elif (ename, o.seq) in rank:
                    ins.then_inc(sems[ename], 1)
            if ename == final_waits_eng:
                for slot, c in sched.dma_cnt.items():
                    if waited.get("dma:" + slot, 0) < c:
                        handle.wait_ge(sems["dma:" + slot], c)

        block = stack.enter_context(nc.Block())

        @block.tensor
        def _(h):
            run_engine("pe", h)

        @block.scalar
        def _(h):
            run_engine("act", h)

        @block.vector
        def _(h):
            run_engine("dve", h)

        @block.gpsimd
        def _(h):
            run_engine("pool", h)

        @block.sync
        def _(h):
            run_engine("sp", h)

        stack.close()
        return {e: len(self.ops[e]) for e in ENGS}


D = 1024
SEQ = 2048
T = 512
NB = T // 128
HEADS = 4
DK = 128
DV = 256
DFF = 3072
NCH_FF = DFF // 128
RMS_EPS = 1e-6
GN_EPS = 1e-6
TWO_PI = 2.0 * np.pi
CW1 = 6.28125
CW2 = float(TWO_PI - 6.28125)

VO = {}
_c = 0
for _n, _w in [("mgb", 16), ("gnw", 8), ("lcw", 32), ("lcb", 8), ("lbr", 8), ("lbi", 8),
               ("lam", 8), ("fcw", 72), ("fcb", 24), ("kdec", 4), ("invf", 1), ("sgn", 1), ("epsq", 4)]:
    VO[_n] = _c
    _c += _w
NV = _c

W_BLOCKS = [("w_in", 14), ("w_ro", 2), ("w_lo", 2), ("w_o", 2), ("w_up", 12), ("w_dn", 6)]


def host_consts():
    lg = np.log1p(-np.power(2.0, -5.0 - np.arange(HEADS, dtype=np.float64)))
    s = DK ** -0.5
    idx = np.arange(128, dtype=np.float64)
    j = idx[:, None]
    i = idx[None, :]
    mask = ((j // 64) <= (i // 64)).astype(np.float64)
    dp = np.zeros((128, HEADS, 128), np.float32)
    qd = np.zeros((128, HEADS, 128), np.float32)
    kdec = np.zeros((128, HEADS), np.float32)
    for h in range(HEADS):
        dp[:, h, :] = (s * np.exp(lg[h] * (np.abs(i - j) - (i + 1.0))) * mask).astype(np.float32)
        qd[:, h, :] = np.exp(lg[h] * (idx + 1.0))[None, :].astype(np.float32)
        kdec[:, h] = (s * np.exp(lg[h] * (127.0 - idx))).astype(np.float32)
    cdec = [float(np.exp(lg[h] * 128.0)) for h in range(HEADS)]
    invf = np.power(np.float32(10000.0), -(np.arange(64, dtype=np.float32) / np.float32(64.0))).astype(np.float32)
    invf = np.concatenate([invf, invf])
    sgn = np.concatenate([np.ones(64, np.float32), -np.ones(64, np.float32)])
    epsq = np.zeros((128, HEADS), np.float32)
    for h in range(HEADS):
        epsq[:, h] = (GN_EPS / np.exp(2.0 * lg[h] * (idx + 1.0))).astype(np.float32)
    return dp, qd, kdec, cdec, invf, sgn, epsq


def build(nseq=4, ntile=4, debug=False):
    nc = bass.Bass("TRN2", target_bir_lowering=False)
    NTOK = nseq * ntile * T
    dpc, qdc, kdecc, cdec, invfc, sgnc, epsqc = host_consts()

    def din(name, shape, dt=F32):
        return nc.dram_tensor(name, shape, dt, kind="ExternalInput").ap()

    x_d = din("x", [NTOK, D])
    pos_d = din("pos", [nseq, ntile * T], I32)
    w_in_d = din("w_in", [D, 7168])
    w_ro_d = din("w_ret_o", [D, D])
    w_lo_d = din("w_lru_o", [D, D])
    w_o_d = din("w_out", [D, D])
    w_up_d = din("ffn_w_up", [D, 2 * DFF])
    w_dn_d = din("ffn_w_down", [DFF, D])
    wr_d = din("lru_w_r", [4, 256, 256])
    wi_d = din("lru_w_i", [4, 256, 256])
    n1_d = din("norm1_w", [1, D])
    n2_d = din("norm2_w", [1, D])
    nf_d = din("norm_f_w", [1, D])
    vecs_d = din("vecs", [128, NV])
    dp_d = din("dpc", [128, HEADS, 128])
    qd_d = din("qdc", [128, HEADS, 128])
    ident_d = din("identc", [128, 128])
    out_d = nc.dram_tensor("out", [NTOK, D], F32, kind="ExternalOutput").ap()

    scr = {}
    for name, nb in W_BLOCKS:
        scr[name] = nc.dram_tensor("scr_" + name, [nb, 128, 8, 512], BF16, kind="Internal").ap()

    S = Sched(nc)
    st = ExitStack()
    dbg_outs = {}

    def sb(name, shape, dt):
        return st.enter_context(nc.sbuf_tensor("sb_" + name, shape, dt))

    xres = [sb("xresA", [128, 4, D], F32), sb("xresB", [128, 4, D], F32)]
    hT = sb("hT", [128, 8, T], BF16)
    BIG = sb("BIG", [128, 24, T], BF16)
    B2 = sb("B2", [128, 16, T], BF16)
    qd = sb("qd", [128, HEADS, T], BF16)
    krot = sb("krot", [128, HEADS, T], BF16)
    kd = sb("kd", [128, HEADS, NB, 128], BF16)
    PTb = sb("PTb", [128, HEADS, T], BF16)
    Sf = sb("Sf", [128, HEADS, DV], F32)
    Sbf = sb("Sbf", [128, HEADS, NB + 1, DV], BF16)
    cos_t = sb("cos_t", [128, T], F32)
    sin_t = sb("sin_t", [128, T], F32)
    NWORK = 12
    work = sb("work", [128, NWORK, T], F32)
    xn = sb("xn", [128, 2, D], BF16)
    rawb = sb("rawb", [128, 4, T + 3], F32)
    halo_l = sb("halo_l", [128, 8, 3], F32)
    hstate = sb("hstate", [128, 8], F32)
    halo_f = sb("halo_f", [128, NCH_FF, 2], F32)
    Wr = sb("Wr", [128, 4, 2, 256], BF16)
    Wi = sb("Wi", [128, 4, 2, 256], BF16)
    ident = sb("ident", [128, 128], BF16)
    Dp = sb("Dp", [128, HEADS, 128], F32)
    QD = sb("QD", [128, HEADS, 128], F32)
    vecs = sb("vecs", [128, NV], F32)
    wb1 = sb("wb1", [128, D], BF16)
    wb2 = sb("wb2", [128, D], BF16)
    wfb = sb("wfb", [128, D], F32)
    c1h = sb("c1h", [128, 8], F32)
    hbias = sb("hbias", [128, 32], F32)
    ctmp = sb("ctmp", [128, 8], F32)
    stat = sb("stat", [128, 24], F32)
    st6 = sb("st6", [128, HEADS, 6], F32)
    mv = sb("mv", [128, HEADS, 2], F32)
    gstat = sb("gstat", [128, 3, HEADS], F32)
    posi = sb("posi", [128, T], I32)
    NW = 4
    ring = sb("ring", [128, NW, 8, T], BF16)

    print('sbuf remaining', nc.sbuf_bytes_remaining)
    PS = st.enter_context(nc.psum_tensor("PS", [128, 6, 512], F32))
    PTR = st.enter_context(nc.psum_tensor("PTR", [128, 2, 1024], BF16))

    sg = BIG[:, 0:8, :]
    v_tm = BIG[:, 8:16, :].rearrange("p (s a) n -> p s (a n)", a=2)
    gr = BIG[:, 8:16, :]
    xcb = BIG[:, 16:24, :]
    mixT = BIG[:, 16:24, :]
    pf = BIG[:]
    gl = B2[:, 0:8, :]
    gy = B2[:, 8:16, :]

    st_ = {"ps": 0, "tr": 0, "wk": 0, "ring": 0, "cast": 0, "gi0": False}
    src_ap = {}

    def aps(*xs):
        return [a for a in xs if not isinstance(a, (int, float)) and a is not None]

    def ACT(out, in_, func, bias=None, scale=None, accum=None):
        kw = {}
        if bias is not None:
            kw["bias"] = bias
        if scale is not None:
            kw["scale"] = scale
        if accum is not None:
            kw["accum_out"] = accum
        S.op("act", lambda h: h.activation(out=out, in_=in_, func=func, **kw),
             reads=aps(in_, bias, scale), writes=aps(out, accum))

    def _e(eng):
        return "dve" if (eng == "pool" and st_.get("gi0")) else eng

    def TT(eng, out, in0, in1, op):
        eng = _e(eng)
        S.op(eng, lambda h: h.tensor_tensor(out=out, in0=in0, in1=in1, op=op), reads=[in0, in1], writes=[out])

    def TS(eng, out, in0, s1, s2, op0, op1=None):
        eng = _e(eng)
        if op1 is None:
            S.op(eng, lambda h: h.tensor_scalar(out=out, in0=in0, scalar1=s1, scalar2=None, op0=op0),
                 reads=aps(in0, s1), writes=[out])
        else:
            S.op(eng, lambda h: h.tensor_scalar(out=out, in0=in0, scalar1=s1, scalar2=s2, op0=op0, op1=op1),
                 reads=aps(in0, s1, s2), writes=[out])

    def STT(out, in0, scalar, in1, op0, op1):
        S.op("dve", lambda h: h.scalar_tensor_tensor(out=out, in0=in0, scalar=scalar, in1=in1, op0=op0, op1=op1),
             reads=aps(in0, scalar, in1), writes=[out])

    def COPY(eng, out, in_):
        eng = _e(eng)
        if eng == "act":
            S.op("act", lambda h: h.copy(out=out, in_=in_), reads=[in_], writes=[out])
        else:
            S.op(eng, lambda h: h.tensor_copy(out=out, in_=in_), reads=[in_], writes=[out])

    def MEMSET(eng, out, val):
        eng = _e(eng)
        S.op(eng, lambda h: h.memset(out, val), writes=[out])

    def MM(out, lhsT, rhs, start, stop):
        S.op("pe", lambda h: h.matmul(out, lhsT=lhsT, rhs=rhs, start=start, stop=stop), reads=[lhsT, rhs], writes=[out])

    def TR(out, in_):
        S.op("pe", lambda h: h.transpose(out, in_, ident[:]), reads=[in_, ident[:]], writes=[out])

    def DMA(eng, out, in_, slot, reads=(), writes=()):
        S.op(eng, lambda h: h.dma_start(out=out, in_=in_), reads=list(reads), writes=list(writes), dma_slot=slot)

    def dbg(name, ap):
        if not debug:
            return
        o = nc.dram_tensor("dbg_" + name, list(ap.shape), ap.dtype, kind="ExternalOutput").ap()
        dbg_outs[name] = o
        DMA("pool", o, ap, "dbg_" + name, reads=[ap])


    def ps_bank():
        b = st_["ps"] % 6
        st_["ps"] += 1
        return PS[:, b, :]

    def ps_pair():
        if st_["ps"] % 2:
            st_["ps"] += 1
        b = st_["ps"] % 6
        st_["ps"] += 2
        return PS[:, b:b + 2, :].rearrange("p a n -> p (a n)")

    def tr_bank():
        b = st_["tr"] % 2
        st_["tr"] += 1
        return PTR[:, b, :]

    def wk():
        b = st_["wk"] % NWORK
        st_["wk"] += 1
        return work[:, b, :]

    def wload(name, blk):
        slot = st_["ring"] % NW
        st_["ring"] += 1
        dst = ring[:, slot, :, :]
        if st_["gi0"]:
            DMA("pool", dst, src_ap[(name, blk)], "cring%d" % slot, writes=[dst])
            DMA("sp", scr[name][blk], dst, "rst%d" % slot, reads=[dst], writes=[("scr_" + name, blk, blk + 1)])
        else:
            DMA("sp", dst, scr[name][blk], "ring%d" % slot, reads=[("scr_" + name, blk, blk + 1)], writes=[dst])
        return dst

    def cload(dst, src, i):
        DMA("pool", dst, src, "c%d" % i, writes=[dst])

    def vcol(name, i):
        o = VO[name] + i
        return vecs[:, o:o + 1]

    def x_load(gi):
        xr = xres[gi % 2]
        for sub in range(4):
            r0 = gi * T + sub * 128
            DMA("pool", xr[:, sub, :], x_d[r0:r0 + 128, :], "x%d_%d" % (gi % 2, sub), writes=[xr[:, sub, :]])

    def pos_load(gi):
        si, ti = divmod(gi, ntile)
        DMA("pool", posi[:], pos_d[si:si + 1, ti * T:(ti + 1) * T].partition_broadcast(128), "pos", writes=[posi[:]])

    cload(vecs[:], vecs_d, 0)
    pos_load(0)
    x_load(0)
    cload(ident[:], ident_d, 1)
    cload(wb1[:], n1_d.partition_broadcast(128), 4)
    cload(Dp[:], dp_d, 2)
    cload(QD[:], qd_d, 3)
    cload(wb2[:], n2_d.partition_broadcast(128), 5)
    cload(wfb[:], nf_d.partition_broadcast(128), 6)
    lam = vecs[:, VO["lam"]:VO["lam"] + 8]
    ACT(ctmp[:], lam, AF.Exp, scale=-1.0)
    ACT(ctmp[:], ctmp[:], AF.Ln, bias=1.0)
    TS("dve", c1h[:], ctmp[:], -4.0, None, ALU.mult)
    TS("dve", hbias[:, 0:8], vecs[:, VO["lbr"]:VO["lbr"] + 8], 0.5, None, ALU.mult)
    TS("dve", hbias[:, 8:16], vecs[:, VO["lbi"]:VO["lbi"] + 8], 0.5, None, ALU.mult)
    TS("dve", hbias[:, 16:32], vecs[:, VO["mgb"]:VO["mgb"] + 16], 0.5, None, ALU.mult)

    def cast_blk(name, blk, ap):
        src_ap[(name, blk)] = ap

    for b in range(14):
        cast_blk("w_in", b, w_in_d[:, b * 512:(b + 1) * 512].rearrange("(kc p) n -> p kc n", p=128))
    DMA("pool", Wr[:], wr_d.rearrange("n (kk p) j -> p n kk j", p=128), "c7", writes=[Wr[:]])
    DMA("pool", Wi[:], wi_d.rearrange("n (kk p) j -> p n kk j", p=128), "c8", writes=[Wi[:]])
    for b in range(2):
        cast_blk("w_ro", b, w_ro_d[:, b * 512:(b + 1) * 512].rearrange("(kc p) n -> p kc n", p=128))
        cast_blk("w_lo", b, w_lo_d[:, b * 512:(b + 1) * 512].rearrange("(kc p) n -> p kc n", p=128))
    for b in range(2):
        cast_blk("w_o", b, w_o_d[:, b * 512:(b + 1) * 512].rearrange("(kc p) n -> p kc n", p=128))
    for g in range(6):
        cast_blk("w_up", g, w_up_d[:, g * 512:(g + 1) * 512].rearrange("(kc p) n -> p kc n", p=128))
        cast_blk("w_up", 6 + g, w_up_d[:, DFF + g * 512:DFF + (g + 1) * 512].rearrange("(kc p) n -> p kc n", p=128))
    for half in range(2):
        for kg in range(3):
            cast_blk("w_dn", half * 3 + kg,
                     w_dn_d[kg * 1024:(kg + 1) * 1024, half * 512:(half + 1) * 512].rearrange("(kc p) n -> p kc n", p=128))

    def norm_stats(xr):
        sqrt_warm()
        for sub in range(4):
            ACT(xn[:, sub % 2, :], xr[:, sub, :], AF.Square, accum=stat[:, sub:sub + 1])
        ACT(stat[:, 4:8], stat[:, 0:4], AF.Sqrt, scale=1.0 / D, bias=RMS_EPS)
        S.op("dve", lambda h: h.reciprocal(out=stat[:, 8:12], in_=stat[:, 4:8]), reads=[stat[:, 4:8]], writes=[stat[:, 8:12]])

    def norm_scale(xr, wb):
        for sub in range(4):
            xs = xn[:, sub % 2, :]
            STT(xs, xr[:, sub, :], stat[:, 8 + sub:9 + sub], wb[:], ALU.mult, ALU.mult)
            tb = tr_bank()
            for c in range(8):
                TR(tb[:, c * 128:(c + 1) * 128], xs[:, c * 128:(c + 1) * 128])
            COPY("act", hT[:, :, sub * 128:(sub + 1) * 128], tb.rearrange("p (c n) -> p c n", n=128))

    def sqrt_warm():
        ACT(stat[:, 23:24], vcol("sgn", 0), AF.Sqrt, scale=0.0, bias=1.0)

    def tables(gi):
        ang = wk()
        ki = wk().bitcast(I32)
        TS("dve", ang, posi[:], vcol("invf", 0), None, ALU.mult)
        TS("dve", ki, ang, 1.0 / TWO_PI, None, ALU.mult)
        r1 = wk()
        STT(r1, ki, -CW1, ang, ALU.mult, ALU.add)
        STT(r1, ki, -CW2, r1, ALU.mult, ALU.add)
        m = wk()
        TS("dve", m, r1, float(np.pi), -TWO_PI, ALU.is_gt, ALU.mult)
        TT("dve", r1, r1, m, ALU.add)
        TS("dve", m, r1, -float(np.pi), TWO_PI, ALU.is_lt, ALU.mult)
        TT("dve", r1, r1, m, ALU.add)
        ACT(sin_t[:], r1, AF.Sin, scale=vcol("sgn", 0))
        cc = wk()
        TS("dve", cc, r1, float(np.pi / 2), None, ALU.add)
        TS("dve", m, cc, float(np.pi), -TWO_PI, ALU.is_gt, ALU.mult)
        TT("dve", cc, cc, m, ALU.add)
        ACT(cos_t[:], cc, AF.Sin)
        if gi + 1 < nseq * ntile:
            pos_load(gi + 1)

    def pre_phase(gi):
        tables(gi)
        norm_stats(xres[gi % 2])
        norm_scale(xres[gi % 2], wb1)

    def proj_fm(wblk, j):
        ps = ps_bank()
        for kc in range(8):
            MM(ps, wblk[:, kc, j * 128:(j + 1) * 128], hT[:, kc, :], kc == 0, kc == 7)
        return ps

    def rotary(ps, dst, qdh=None):
        qs = wk()
        COPY("act", qs, ps)
        t1 = wk()
        t2 = wk()
        TT("dve", t1, qs, cos_t[:], ALU.mult)
        TT("dve", t2[0:64, :], qs[64:128, :], sin_t[64:128, :], ALU.mult)
        TT("dve", t2[64:128, :], qs[0:64, :], sin_t[0:64, :], ALU.mult)
        TT("dve", dst, t1, t2, ALU.add)

    lcw = lambda k, c: vcol("lcw", k * 8 + c)
    fcw = lambda k, c: vcol("fcw", k * NCH_FF + c)

    def main_phase(gi):
        si, ti = divmod(gi, ntile)
        first = (ti == 0)
        xr_all = xres[gi % 2]
        if gi + 1 < nseq * ntile:
            x_load(gi + 1)
        if first:
            MEMSET("pool", Sf[:], 0.0)
            MEMSET("pool", Sbf[:, :, 0, :], 0.0)
            MEMSET("pool", halo_l[:], 0.0)
            MEMSET("pool", halo_f[:], 0.0)
        if debug and gi == 0:
            dbg("cos", cos_t[:])
            dbg("sin", sin_t[:])
            dbg("h1T", hT[:])
        def v_half(half):
            wv = wload("w_in", 2 + half)
            for sub in range(4):
                ps = ps_bank()
                for kc in range(8):
                    MM(ps, hT[:, kc, sub * 128:(sub + 1) * 128], wv[:, kc, :], kc == 0, kc == 7)
                COPY("act", v_tm[:, sub, half * 512:(half + 1) * 512], ps)

        wq = wload("w_in", 0)
        for h in range(HEADS):
            ps = proj_fm(wq, h)
            rotary(ps, qd[:, h, :], QD[:, h, :])
        v_half(0)
        wk_ = wload("w_in", 1)
        for h in range(HEADS):
            ps = proj_fm(wk_, h)
            rotary(ps, krot[:, h, :])
        v_half(1)
        def xlru_half(half):
            wx = wload("w_in", 6 + half)
            todo = []
            for j in range(4):
                c = half * 4 + j
                ps = proj_fm(wx, j)
                xs = rawb[:, c % 4, :]
                COPY("act", xs[:, 3:T + 3], ps)
                COPY("pool", xs[:, 0:3], halo_l[:, c, :])
                todo.append((c, xs))

            def conv():
                for c, xs in todo:
                    if half == 1:
                        acc = wk()
                        tmp = wk()
                        TS("pool", acc, xs[:, 3:T + 3], lcw(3, c), vcol("lcb", c), ALU.mult, ALU.add)
                        for k in (2, 1, 0):
                            TS("pool", tmp, xs[:, k:T + k], lcw(k, c), 0.0, ALU.mult, ALU.add)
                            TT("pool", xcb[:, c, :] if k == 0 else acc, acc, tmp, ALU.add)
                        COPY("pool", halo_l[:, c, :], xs[:, T:T + 3])
                        continue
                    acc = wk()
                    TS("dve", acc, xs[:, 3:T + 3], lcw(3, c), vcol("lcb", c), ALU.mult, ALU.add)
                    STT(acc, xs[:, 2:T + 2], lcw(2, c), acc, ALU.mult, ALU.add)
                    STT(acc, xs[:, 1:T + 1], lcw(1, c), acc, ALU.mult, ALU.add)
                    STT(xcb[:, c, :], xs[:, 0:T], lcw(0, c), acc, ALU.mult, ALU.add)
                    COPY("pool", halo_l[:, c, :], xs[:, T:T + 3])
            return conv

        for hp in range(2):
            tb = tr_bank()
            for hh in range(2):
                h = hp * 2 + hh
                for b in range(NB):
                    TR(tb[:, (hh * NB + b) * 128:(hh * NB + b + 1) * 128], krot[:, h, b * 128:(b + 1) * 128])
            for hh in range(2):
                h = hp * 2 + hh
                ACT(kd[:, h, :, :].rearrange("p b d -> p (b d)"), tb[:, hh * 512:(hh + 1) * 512], AF.Copy,
                    scale=vcol("kdec", h))
        for h in range(HEADS):
            ps = ps_bank()
            for b in range(NB):
                MM(ps[:, b * 128:(b + 1) * 128], krot[:, h, b * 128:(b + 1) * 128], qd[:, h, b * 128:(b + 1) * 128], True, True)
            TT("dve", PTb[:, h, :].rearrange("p (b i) -> p b i", i=128), ps.rearrange("p (b i) -> p b i", i=128),
               Dp[:, h, :].unsqueeze(1).to_broadcast([128, NB, 128]), ALU.mult)
        xlru_half(0)()
        conv1 = xlru_half(1)

        def gret_half(half):
            wg = wload("w_in", 4 + half)
            for j in range(4):
                c = half * 4 + j
                ps = proj_fm(wg, j)
                ACT(sg[:, c, :], ps, AF.Silu)
                TS("dve", sg[:, c, :], sg[:, c, :], vcol("gnw", c), None, ALU.mult)

        upairs = []
        for h in range(HEADS):
            pp = ps_pair()
            for b in range(NB):
                MM(pp[:, b * DV:(b + 1) * DV], kd[:, h, b, :], v_tm[:, b, h * DV:(h + 1) * DV], True, True)
            upairs.append(pp)
            if h % 2 == 1:
                for b in range(NB):
                    for h2 in (h - 1, h):
                        u_ = upairs[h2][:, b * DV:(b + 1) * DV]
                        STT(Sbf[:, h2, b + 1, :], Sf[:, h2, :], cdec[h2], u_, ALU.mult, ALU.add)
                        STT(Sf[:, h2, :], Sf[:, h2, :], cdec[h2], u_, ALU.mult, ALU.add)
                gret_half(h // 2)
        if debug and gi == 0:
            dbg("qd", qd[:])
            dbg("krot", krot[:])
            dbg("v", v_tm)
            dbg("xcb", xcb)

        wy = [None, None]

        def o_mm(b):
            pp = ps_pair()
            for h in range(HEADS):
                o_ = pp[:, h * DV:(h + 1) * DV]
                MM(o_, PTb[:, h, b * 128:(b + 1) * 128], v_tm[:, b, h * DV:(h + 1) * DV], True, False)
                MM(o_, qd[:, h, b * 128:(b + 1) * 128], Sbf[:, h, b, :], False, True)
            return pp

        def o_norm(b, pp):
            for h in range(HEADS):
                S.op("dve", lambda hd, h=h, pp=pp: hd.bn_stats(out=st6[:, h, :], in_=pp[:, h * DV:(h + 1) * DV]),
                     reads=[pp[:, h * DV:(h + 1) * DV]], writes=[st6[:, h, :]])
            for h in range(HEADS):
                S.op("dve", lambda hd, h=h: hd.bn_aggr(out=mv[:, h, :], in_=st6[:, h, :]),
                     reads=[st6[:, h, :]], writes=[mv[:, h, :]])
            TT("dve", gstat[:, 0, :], mv[:, :, 1], vecs[:, VO["epsq"]:VO["epsq"] + HEADS], ALU.add)
            ACT(gstat[:, 0, :], gstat[:, 0, :], AF.Sqrt)
            S.op("dve", lambda hd: hd.reciprocal(out=gstat[:, 1, :], in_=gstat[:, 0, :]), reads=[gstat[:, 0, :]], writes=[gstat[:, 1, :]])
            STT(gstat[:, 2, :], mv[:, :, 0], -1.0, gstat[:, 1, :], ALU.mult, ALU.mult)
            ob = xn[:, b % 2, :]
            for h in range(HEADS):
                ACT(ob[:, h * DV:(h + 1) * DV], pp[:, h * DV:(h + 1) * DV], AF.Identity,
                    bias=gstat[:, 2, h:h + 1], scale=gstat[:, 1, h:h + 1])
            if debug and gi == 0 and b == 0:
                dbg("onb0", ob)

        def o_tr(b):
            ob = xn[:, b % 2, :]
            tb = tr_bank()
            for c in range(8):
                TR(tb[:, c * 128:(c + 1) * 128], ob[:, c * 128:(c + 1) * 128])
            og_b = sg[:, :, b * 128:(b + 1) * 128]
            TT("dve", og_b, tb.rearrange("p (c n) -> p c n", n=128), og_b, ALU.mult)

        def ylru(half):
            wyb = wload("w_in", 8 + half)
            for j in range(4):
                c = half * 4 + j
                ps = proj_fm(wyb, j)
                ACT(gy[:, c, :], ps, AF.Gelu_apprx_tanh)

        pp0 = o_mm(0)
        o_norm(0, pp0)
        pp1 = o_mm(1)
        o_norm(1, pp1)
        conv1()
        ylru(0)
        o_tr(0)
        pp2 = o_mm(2)
        o_norm(2, pp2)
        o_tr(1)
        pp3 = o_mm(3)
        o_norm(3, pp3)
        ylru(1)
        o_tr(2)
        o_tr(3)
        if not (ti == ntile - 1):
            COPY("pool", Sbf[:, :, 0, :], Sbf[:, :, NB, :])

        mg_list = [(which, half) for which in range(2) for half in range(2)]

        def merge_gate_block(which, half):
            dst = gr if which == 0 else gl
            wg = wload("w_in", 10 + which * 2 + half)
            for j in range(4):
                c = half * 4 + j
                ps = proj_fm(wg, j)
                ACT(dst[:, c, :], ps, AF.Tanh, bias=hbias[:, 16 + which * 8 + c:17 + which * 8 + c], scale=0.5)
                TS("pool", dst[:, c, :], dst[:, c, :], 0.5, 0.5, ALU.mult, ALU.add)

        for cp in range(4):
            if cp < 3:
                merge_gate_block(*mg_list[cp])
            bufs = []
            for c in (2 * cp, 2 * cp + 1):
                n = c // 2
                cc_ = c % 2
                psr = ps_bank()
                for kk in range(2):
                    MM(psr, Wr[:, n, kk, cc_ * 128:(cc_ + 1) * 128], xcb[:, 2 * n + kk, :], kk == 0, kk == 1)
                psi = ps_bank()
                for kk in range(2):
                    MM(psi, Wi[:, n, kk, cc_ * 128:(cc_ + 1) * 128], xcb[:, 2 * n + kk, :], kk == 0, kk == 1)
                r_ = wk()
                i_ = wk()
                a_ = wk()
                ACT(r_, psr, AF.Tanh, bias=hbias[:, c:c + 1], scale=0.5)
                ACT(i_, psi, AF.Tanh, bias=hbias[:, 8 + c:9 + c], scale=0.5)
                ACT(a_, r_, AF.Exp, bias=c1h[:, c:c + 1], scale=c1h[:, c:c + 1])
                TT("dve", r_, a_, a_, ALU.mult)
                bufs.append((c, r_, i_, a_))
            for (c, r_, i_, a_) in bufs:
                ACT(r_, r_, AF.Sqrt, scale=-1.0, bias=1.0)
            for (c, r_, i_, a_) in bufs:
                STT(i_, i_, 1.0, r_, ALU.add, ALU.mult)
                STT(i_, i_, 0.5, xcb[:, c, :], ALU.mult, ALU.mult)
                hl = wk()
                init = 0.0 if first else hstate[:, c:c + 1]
                S.op("dve", lambda hd, hl=hl, a_=a_, i_=i_, init=init: hd.tensor_tensor_scan(
                    out=hl, data0=a_, data1=i_, initial=init, op0=ALU.mult, op1=ALU.add),
                    reads=aps(a_, i_, init), writes=[hl])
                COPY("pool", hstate[:, c:c + 1], hl[:, T - 1:T])
                if debug and gi == 0 and c == 0:
                    dbg("hl0", hl)
                TT("dve", gy[:, c, :], hl, gy[:, c, :], ALU.mult)
        merge_gate_block(*mg_list[3])
        sqrt_warm()
        if debug and gi == 0:
            dbg("og", sg)
            dbg("pg", gy)
        for half in range(2):
            wro = wload("w_ro", half)
            wlo = wload("w_lo", half)
            m1s = []
            for j in range(4):
                oc = half * 4 + j
                psa = ps_bank()
                for kc in range(8):
                    MM(psa, wro[:, kc, j * 128:(j + 1) * 128], sg[:, kc, :], kc == 0, kc == 7)
                m1 = wk()
                TT("dve", m1, psa, gr[:, oc, :], ALU.mult)
                m1s.append(m1)
            for j in range(4):
                oc = half * 4 + j
                psb = ps_bank()
                for kc in range(8):
                    MM(psb, wlo[:, kc, j * 128:(j + 1) * 128], gy[:, kc, :], kc == 0, kc == 7)
                m2 = wk()
                TT("dve", m2, psb, gl[:, oc, :], ALU.mult)
                TT("pool", mixT[:, oc, :], m1s[j], m2, ALU.add)
        if debug and gi == 0:
            dbg("mixT", mixT)
        wo = [wload("w_o", 0), wload("w_o", 1)]

        def n2_slot(sub):
            if sub < 2:
                return xn[:, sub, :]
            return gl[:, 4 + (sub - 2) * 2:6 + (sub - 2) * 2, :].rearrange("p a n -> p (a n)")

        def n2_tr(sub):
            xs = n2_slot(sub)
            tb = tr_bank()
            for c in range(8):
                TR(tb[:, c * 128:(c + 1) * 128], xs[:, c * 128:(c + 1) * 128])
            COPY("act", hT[:, :, sub * 128:(sub + 1) * 128], tb.rearrange("p (c n) -> p c n", n=128))

        def n2_stt(sub):
            STT(n2_slot(sub), xr_all[:, sub, :], stat[:, 8 + sub:9 + sub], wb2[:], ALU.mult, ALU.mult)

        for sub in range(4):
            for half in range(2):
                ps = ps_bank()
                for kc in range(8):
                    MM(ps, mixT[:, kc, sub * 128:(sub + 1) * 128], wo[half][:, kc, :], kc == 0, kc == 7)
                xr = xr_all[:, sub, half * 512:(half + 1) * 512]
                TT("dve", xr, xr, ps, ALU.add)
            if sub < 3:
                ACT(gl[:, (sub % 2) * 2:(sub % 2) * 2 + 2, :].rearrange("p a n -> p (a n)"), xr_all[:, sub, :], AF.Square,
                    accum=stat[:, sub:sub + 1])
            if sub == 2:
                ACT(stat[:, 4:7], stat[:, 0:3], AF.Sqrt, scale=1.0 / D, bias=RMS_EPS)
                S.op("dve", lambda h: h.reciprocal(out=stat[:, 8:11], in_=stat[:, 4:7]), reads=[stat[:, 4:7]], writes=[stat[:, 8:11]])
                n2_stt(0)
                n2_stt(1)
        if debug and gi == 0:
            dbg("x1", xr_all[:])
        n2_stt(2)
        n2_tr(0)
        n2_tr(1)
        ACT(gl[:, 2:4, :].rearrange("p a n -> p (a n)"), xr_all[:, 3, :], AF.Square, accum=stat[:, 3:4])
        ACT(stat[:, 7:8], stat[:, 3:4], AF.Sqrt, scale=1.0 / D, bias=RMS_EPS)
        S.op("dve", lambda h: h.reciprocal(out=stat[:, 11:12], in_=stat[:, 7:8]), reads=[stat[:, 7:8]], writes=[stat[:, 11:12]])
        n2_stt(3)
        n2_tr(2)
        n2_tr(3)
        for g in range(6):
            wgt = wload("w_up", g)
            wvl = wload("w_up", 6 + g)
            for j in range(4):
                oc = g * 4 + j
                psg = proj_fm(wgt, j)
                psv = proj_fm(wvl, j)
                gs = rawb[:, oc % 4, :]
                COPY("act", gs[:, 2:T + 2], psg)
                COPY("pool", gs[:, 0:2], halo_f[:, oc, :])
                acc = wk()
                TS("dve", acc, gs[:, 2:T + 2], fcw(2, oc), vcol("fcb", oc), ALU.mult, ALU.add)
                STT(acc, gs[:, 1:T + 1], fcw(1, oc), acc, ALU.mult, ALU.add)
                STT(acc, gs[:, 0:T], fcw(0, oc), acc, ALU.mult, ALU.add)
                COPY("pool", halo_f[:, oc, :], gs[:, T:T + 2])
                ACT(acc, acc, AF.Gelu_apprx_tanh)
                TT("dve", pf[:, oc, :], acc, psv, ALU.mult)
        if debug and gi == 0:
            dbg("pf", pf)

    def down_phase(gi, mid_hook=None):
        xr_all = xres[gi % 2]
        for half in range(2):
            pss = [ps_bank() for _ in range(4)]
            for kg in range(3):
                wd = wload("w_dn", half * 3 + kg)
                for pair in ((0, 1), (2, 3)):
                    for kc in range(8):
                        c = kg * 8 + kc
                        for sub in pair:
                            MM(pss[sub], pf[:, c, sub * 128:(sub + 1) * 128], wd[:, kc, :], c == 0, c == NCH_FF - 1)
                if half == 0 and kg == 2 and mid_hook is not None:
                    mid_hook()
            for sub in range(4):
                xr = xr_all[:, sub, half * 512:(half + 1) * 512]
                TT("dve", xr, xr, pss[sub], ALU.add)
        sqrt_warm()
        for sub in range(4):
            ACT(xn[:, sub % 2, :], xr_all[:, sub, :], AF.Square, accum=stat[:, 12 + sub:13 + sub])
        ACT(stat[:, 16:20], stat[:, 12:16], AF.Sqrt, scale=1.0 / D, bias=RMS_EPS)
        S.op("dve", lambda h: h.reciprocal(out=stat[:, 20:24], in_=stat[:, 16:20]), reads=[stat[:, 16:20]], writes=[stat[:, 20:24]])
        for sub in range(4):
            STT(xr_all[:, sub, :], xr_all[:, sub, :], stat[:, 20 + sub:21 + sub], wfb[:], ALU.mult, ALU.mult)
            r0 = gi * T + sub * 128
            DMA("pool", out_d[r0:r0 + 128, :], xr_all[:, sub, :], "o%d_%d" % (gi % 2, sub), reads=[xr_all[:, sub, :]])

    ntot = nseq * ntile
    pre_phase(0)
    for gi in range(ntot):
        st_["gi0"] = (gi == 0)
        main_phase(gi)
        if gi + 1 < ntot:
            down_phase(gi, mid_hook=lambda gi=gi: pre_phase(gi + 1))
        else:
            down_phase(gi)

    counts = S.emit()
    st.close()
    return nc, counts, dbg_outs


def prep_inputs(inputs, core, nseq=4, ntile=4):
    f = lambda a: np.ascontiguousarray(np.asarray(a), dtype=np.float32)
    dpc, qdc, kdecc, cdec, invfc, sgnc, epsqc = host_consts()
    x = f(inputs["x"])
    ntok = nseq * ntile * T
    xs = x[core * nseq:(core + 1) * nseq, :ntile * T, :].reshape(ntok, D)
    pos = np.ascontiguousarray(np.asarray(inputs["positions"])[core * nseq:(core + 1) * nseq, :ntile * T].astype(np.int32))
    fm = lambda v, nch: f(v).reshape(nch, 128).T
    cols = {
        "mgb": np.concatenate([fm(inputs["merge_gate_b"][0, 0], 8), fm(inputs["merge_gate_b"][0, 1], 8)], axis=1),
        "gnw": fm(inputs["ret_gn_w"][0], 8),
        "lcw": np.concatenate([fm(inputs["lru_conv_w"][0, k], 8) for k in range(4)], axis=1),
        "lcb": fm(inputs["lru_conv_b"][0], 8),
        "lbr": fm(np.asarray(inputs["lru_b_r"][0]).reshape(-1), 8),
        "lbi": fm(np.asarray(inputs["lru_b_i"][0]).reshape(-1), 8),
        "lam": fm(inputs["lru_lambda"][0], 8),
        "fcw": np.concatenate([fm(inputs["ffn_conv_w"][0, k], NCH_FF) for k in range(3)], axis=1),
        "fcb": fm(inputs["ffn_conv_b"][0], NCH_FF),
        "kdec": kdecc,
        "invf": invfc[:, None],
        "sgn": sgnc[:, None],
        "epsq": epsqc,
    }
    vecs = np.zeros((128, NV), np.float32)
    for k, o in VO.items():
        a = cols[k]
        vecs[:, o:o + a.shape[1]] = a
    return {
        "x": np.ascontiguousarray(xs),
        "pos": pos,
        "w_in": f(inputs["w_in"][0]),
        "w_ret_o": f(inputs["w_ret_o"][0]),
        "w_lru_o": f(inputs["w_lru_o"][0]),
        "w_out": f(inputs["w_out"][0]),
        "ffn_w_up": f(inputs["ffn_w_up"][0]),
        "ffn_w_down": f(inputs["ffn_w_down"][0]),
        "lru_w_r": f(inputs["lru_w_r"][0]),
        "lru_w_i": f(inputs["lru_w_i"][0]),
        "norm1_w": f(inputs["norm1_w"][0])[None, :],
        "norm2_w": f(inputs["norm2_w"][0])[None, :],
        "norm_f_w": f(inputs["norm_f_w"]).reshape(1, D),
        "vecs": vecs,
        "dpc": dpc,
        "qdc": qdc,
        "identc": np.eye(128, dtype=np.float32),
    }


_CACHE = {}


def kernel(**inputs):
    ncores = 8
    if "nc" not in _CACHE:
        _CACHE["nc"] = build(4, 4)[0]
    nc = _CACHE["nc"]
    in_maps = [prep_inputs(inputs, c) for c in range(ncores)]
    res = run_bass_kernel_spmd(nc, in_maps, core_ids=list(range(ncores)))
    outs = [np.asarray(r["out"], dtype=np.float32).reshape(4, SEQ, D) for r in res.results]
    return np.concatenate(outs, axis=0)
```

```python
import numpy as np
import ml_dtypes
from contextlib import ExitStack
import concourse.bass as bass
import concourse.mybir as mybir
from concourse.bass_utils import run_bass_kernel_spmd

F32 = mybir.dt.float32
BF16 = mybir.dt.bfloat16
I32 = mybir.dt.int32
AF = mybir.ActivationFunctionType
ALU = mybir.AluOpType
_DT_SIZE = {F32: 4, BF16: 2, I32: 4}
ENGS = ("pe", "act", "dve", "pool", "sp")


def ap_region(ap):
    es = _DT_SIZE[ap.dtype]
    pairs = ap.ap
    pstep = pairs[0][0]
    off = ap.offset
    lo = off % pstep if pstep > 0 else off
    span = 0
    for st, cnt in pairs[1:]:
        span += abs(st) * (cnt - 1)
    lo_b, hi_b = lo * es, (lo + span + 1) * es
    if ap.tensor.name in ("PS", "PTR"):
        lo_b = (lo_b // 2048) * 2048
        hi_b = ((hi_b + 2047) // 2048) * 2048
    return (ap.tensor.name, lo_b, hi_b)


class Op:
    __slots__ = ("eng", "fn", "seq", "waits", "token", "dma_sem")

    def __init__(self, eng, fn):
        self.eng = eng
        self.fn = fn
        self.waits = {}
        self.token = None
        self.dma_sem = None


class Sched:
    def __init__(self, nc):
        self.nc = nc
        self.ops = {e: [] for e in ENGS}
        self.track = {}
        self.dma_cnt = {}

    def _segs(self, name, lo, hi):
        lst = self.track.get(name)
        if lst is None:
            lst = []
        out = []
        new = []
        cur = lo
        lst.sort(key=lambda r: r[0])
        for r in lst:
            if r[1] <= lo or r[0] >= hi:
                new.append(r)
                continue
            if r[0] < lo:
                new.append([r[0], lo, r[2], dict(r[3])])
                r = [lo, r[1], r[2], r[3]]
            if r[1] > hi:
                new.append([hi, r[1], r[2], dict(r[3])])
                r = [r[0], hi, r[2], r[3]]
            if r[0] > cur:
                g = [cur, r[0], None, {}]
                new.append(g)
                out.append(g)
            new.append(r)
            out.append(r)
            cur = r[1]
        if cur < hi:
            g = [cur, hi, None, {}]
            new.append(g)
            out.append(g)
        self.track[name] = new
        return out

    @staticmethod
    def _add_wait(op, tok):
        if tok is None:
            return
        s, v = tok
        if op.waits.get(s, 0) < v:
            op.waits[s] = v

    def op(self, eng, fn, reads=(), writes=(), dma_slot=None):
        o = Op(eng, fn)
        lst = self.ops[eng]
        o.seq = len(lst) + 1
        if dma_slot is not None:
            c = self.dma_cnt.get(dma_slot, 0) + 16
            self.dma_cnt[dma_slot] = c
            o.dma_sem = dma_slot
            o.token = ("dma:" + dma_slot, c)
        else:
            o.token = (eng, o.seq)
        s, v = o.token
        rregs = [r if isinstance(r, tuple) else ap_region(r) for r in reads]
        wregs = [w if isinstance(w, tuple) else ap_region(w) for w in writes]
        for reg in rregs:
            for seg in self._segs(*reg):
                self._add_wait(o, seg[2])
        for reg in wregs:
            for seg in self._segs(*reg):
                self._add_wait(o, seg[2])
                for s2, v2 in seg[3].items():
                    self._add_wait(o, (s2, v2))
        for reg in rregs:
            for seg in self._segs(*reg):
                if seg[3].get(s, 0) < v:
                    seg[3][s] = v
        for reg in wregs:
            for seg in self._segs(*reg):
                seg[2] = o.token
                seg[3] = {}
        lst.append(o)
        return o

    def emit(self, final_waits_eng="sp", same_engine_sync=("act", "dve", "pool")):
        nc = self.nc
        need = {e: set() for e in ENGS}
        for e in ENGS:
            for o in self.ops[e]:
                for s, v in list(o.waits.items()):
                    if s.startswith("dma:"):
                        if s == "dma:" + str(o.dma_sem) and v >= o.token[1]:
                            del o.waits[s]
                        continue
                    if s == e and (e not in same_engine_sync or v >= o.seq):
                        del o.waits[s]
                        continue
                    need[s].add(v)
        rank = {}
        for e in ENGS:
            for i, v in enumerate(sorted(need[e])):
                rank[(e, v)] = i + 1
        stack = ExitStack()
        sems = {}
        for e in ENGS:
            sems[e] = stack.enter_context(nc.semaphore("s_" + e))
        for slot in self.dma_cnt:
            sems["dma:" + slot] = stack.enter_context(nc.semaphore("d_" + slot))
        sched = self

        def run_engine(ename, handle):
            waited = {}
            for o in sched.ops[ename]:
                for s, v in o.waits.items():
                    val = v if s.startswith("dma:") else rank[(s, v)]
                    if waited.get(s, 0) >= val:
                        continue
                    handle.wait_ge(sems[s], val)
                    waited[s] = val
                ins = o.fn(handle)
                if o.dma_sem is not None:
                    ins.then_inc(sems["dma:" + o.dma_sem], 16)
                elif (ename, o.seq) in rank:
                    ins.then_inc(sems[ename], 1)
            if ename == final_waits_eng:
                for slot, c in sched.dma_cnt.items():
                    if waited.get("dma:" + slot, 0) < c:
                        handle.wait_ge(sems["dma:" + slot], c)

        block = stack.enter_context(nc.Block())

        @block.tensor
        def _(h):
            run_engine("pe", h)

        @block.scalar
        def _(h):
            run_engine("act", h)

        @block.vector
        def _(h):
            run_engine("dve", h)

        @block.gpsimd
        def _(h):
            run_engine("pool", h)

        @block.sync
        def _(h):
            run_engine("sp", h)

        stack.close()
        return {e: len(self.ops[e]) for e in ENGS}


D = 1024
SEQ = 2048
T = 512
NB = T // 128
HEADS = 4
DK = 128
DV = 256
DFF = 3072
NCH_FF = DFF // 128
RMS_EPS = 1e-6
GN_EPS = 1e-6
TWO_PI = 2.0 * np.pi
CW1 = 6.28125
CW2 = float(TWO_PI - 6.28125)

VO = {}
_c = 0
for _n, _w in [("mgb", 16), ("gnw", 8), ("lcw", 32), ("lcb", 8), ("lbr", 8), ("lbi", 8),
               ("lam", 8), ("fcw", 72), ("fcb", 24), ("kdec", 4), ("invf", 1), ("sgn", 1), ("epsq", 4)]:
    VO[_n] = _c
    _c += _w
NV = _c

W_BLOCKS = [("w_in", 14), ("w_ro", 2), ("w_lo", 2), ("w_o", 2), ("w_up", 12), ("w_dn", 6)]


def host_consts():
    lg = np.log1p(-np.power(2.0, -5.0 - np.arange(HEADS, dtype=np.float64)))
    s = DK ** -0.5
    idx = np.arange(128, dtype=np.float64)
    j = idx[:, None]
    i = idx[None, :]
    mask = ((j // 64) <= (i // 64)).astype(np.float64)
    dp = np.zeros((128, HEADS, 128), np.float32)
    qd = np.zeros((128, HEADS, 128), np.float32)
    kdec = np.zeros((128, HEADS), np.float32)
    for h in range(HEADS):
        dp[:, h, :] = (s * np.exp(lg[h] * (np.abs(i - j) - (i + 1.0))) * mask).astype(np.float32)
        qd[:, h, :] = np.exp(lg[h] * (idx + 1.0))[None, :].astype(np.float32)
        kdec[:, h] = (s * np.exp(lg[h] * (127.0 - idx))).astype(np.float32)
    cdec = [float(np.exp(lg[h] * 128.0)) for h in range(HEADS)]
    invf = np.power(np.float32(10000.0), -(np.arange(64, dtype=np.float32) / np.float32(64.0))).astype(np.float32)
    invf = np.concatenate([invf, invf])
    sgn = np.concatenate([np.ones(64, np.float32), -np.ones(64, np.float32)])
    epsq = np.zeros((128, HEADS), np.float32)
    for h in range(HEADS):
        epsq[:, h] = (GN_EPS / np.exp(2.0 * lg[h] * (idx + 1.0))).astype(np.float32)
    return dp, qd, kdec, cdec, invf, sgn, epsq


def build(nseq=4, ntile=4, debug=False):
    nc = bass.Bass("TRN2", target_bir_lowering=False)
    NTOK = nseq * ntile * T
    dpc, qdc, kdecc, cdec, invfc, sgnc, epsqc = host_consts()

    def din(name, shape, dt=F32):
        return nc.dram_tensor(name, shape, dt, kind="ExternalInput").ap()

    x_d = din("x", [NTOK, D])
    pos_d = din("pos", [nseq, ntile * T], I32)
    w_in_d = din("w_in", [D, 7168])
    w_ro_d = din("w_ret_o", [D, D])
    w_lo_d = din("w_lru_o", [D, D])
    w_o_d = din("w_out", [D, D])
    w_up_d = din("ffn_w_up", [D, 2 * DFF])
    w_dn_d = din("ffn_w_down", [DFF, D])
    wr_d = din("lru_w_r", [4, 256, 256])
    wi_d = din("lru_w_i", [4, 256, 256])
    n1_d = din("norm1_w", [1, D])
    n2_d = din("norm2_w", [1, D])
    nf_d = din("norm_f_w", [1, D])
    vecs_d = din("vecs", [128, NV])
    dp_d = din("dpc", [128, HEADS, 128])
    qd_d = din("qdc", [128, HEADS, 128])
    ident_d = din("identc", [128, 128])
    out_d = nc.dram_tensor("out", [NTOK, D], F32, kind="ExternalOutput").ap()

    scr = {}
    for name, nb in W_BLOCKS:
        scr[name] = nc.dram_tensor("scr_" + name, [nb, 128, 8, 512], BF16, kind="Internal").ap()

    S = Sched(nc)
    st = ExitStack()
    dbg_outs = {}

    def sb(name, shape, dt):
        return st.enter_context(nc.sbuf_tensor("sb_" + name, shape, dt))

    xres = [sb("xresA", [128, 4, D], F32), sb("xresB", [128, 4, D], F32)]
    hT = sb("hT", [128, 8, T], BF16)
    BIG = sb("BIG", [128, 24, T], BF16)
    B2 = sb("B2", [128, 16, T], BF16)
    qd = sb("qd", [128, HEADS, T], BF16)
    krot = sb("krot", [128, HEADS, T], BF16)
    kd = sb("kd", [128, HEADS, NB, 128], BF16)
    PTb = sb("PTb", [128, HEADS, T], BF16)
    Sf = sb("Sf", [128, HEADS, DV], F32)
    Sbf = sb("Sbf", [128, HEADS, NB + 1, DV], BF16)
    cos_t = sb("cos_t", [128, T], F32)
    sin_t = sb("sin_t", [128, T], F32)
    NWORK = 12
    work = sb("work", [128, NWORK, T], F32)
    xn = sb("xn", [128, 2, D], BF16)
    rawb = sb("rawb", [128, 4, T + 3], F32)
    halo_l = sb("halo_l", [128, 8, 3], F32)
    hstate = sb("hstate", [128, 8], F32)
    halo_f = sb("halo_f", [128, NCH_FF, 2], F32)
    Wr = sb("Wr", [128, 4, 2, 256], BF16)
    Wi = sb("Wi", [128, 4, 2, 256], BF16)
    ident = sb("ident", [128, 128], BF16)
    Dp = sb("Dp", [128, HEADS, 128], F32)
    QD = sb("QD", [128, HEADS, 128], F32)
    vecs = sb("vecs", [128, NV], F32)
    wb1 = sb("wb1", [128, D], BF16)
    wb2 = sb("wb2", [128, D], BF16)
    wfb = sb("wfb", [128, D], F32)
    c1h = sb("c1h", [128, 8], F32)
    hbias = sb("hbias", [128, 32], F32)
    ctmp = sb("ctmp", [128, 8], F32)
    stat = sb("stat", [128, 24], F32)
    st6 = sb("st6", [128, HEADS, 6], F32)
    mv = sb("mv", [128, HEADS, 2], F32)
    gstat = sb("gstat", [128, 3, HEADS], F32)
    posi = sb("posi", [128, T], I32)
    NW = 4
    ring = sb("ring", [128, NW, 8, T], BF16)

    print('sbuf remaining', nc.sbuf_bytes_remaining)
    PS = st.enter_context(nc.psum_tensor("PS", [128, 6, 512], F32))
    PTR = st.enter_context(nc.psum_tensor("PTR", [128, 2, 1024], BF16))

    sg = BIG[:, 0:8, :]
    v_tm = BIG[:, 8:16, :].rearrange("p (s a) n -> p s (a n)", a=2)
    gr = BIG[:, 8:16, :]
    xcb = BIG[:, 16:24, :]
    mixT = BIG[:, 16:24, :]
    pf = BIG[:]
    gl = B2[:, 0:8, :]
    gy = B2[:, 8:16, :]

    st_ = {"ps": 0, "tr": 0, "wk": 0, "ring": 0, "cast": 0, "gi0": False}
    src_ap = {}

    def aps(*xs):
        return [a for a in xs if not isinstance(a, (int, float)) and a is not None]

    def ACT(out, in_, func, bias=None, scale=None, accum=None):
        kw = {}
        if bias is not None:
            kw["bias"] = bias
        if scale is not None:
            kw["scale"] = scale
        if accum is not None:
            kw["accum_out"] = accum
        S.op("act", lambda h: h.activation(out=out, in_=in_, func=func, **kw),
             reads=aps(in_, bias, scale), writes=aps(out, accum))

    def _e(eng):
        return "dve" if (eng == "pool" and st_.get("gi0")) else eng

    def TT(eng, out, in0, in1, op):
        eng = _e(eng)
        S.op(eng, lambda h: h.tensor_tensor(out=out, in0=in0, in1=in1, op=op), reads=[in0, in1], writes=[out])

    def TS(eng, out, in0, s1, s2, op0, op1=None):
        eng = _e(eng)
        if op1 is None:
            S.op(eng, lambda h: h.tensor_scalar(out=out, in0=in0, scalar1=s1, scalar2=None, op0=op0),
                 reads=aps(in0, s1), writes=[out])
        else:
            S.op(eng, lambda h: h.tensor_scalar(out=out, in0=in0, scalar1=s1, scalar2=s2, op0=op0, op1=op1),
                 reads=aps(in0, s1, s2), writes=[out])

    def STT(out, in0, scalar, in1, op0, op1):
        S.op("dve", lambda h: h.scalar_tensor_tensor(out=out, in0=in0, scalar=scalar, in1=in1, op0=op0, op1=op1),
             reads=aps(in0, scalar, in1), writes=[out])

    def COPY(eng, out, in_):
        eng = _e(eng)
        if eng == "act":
            S.op("act", lambda h: h.copy(out=out, in_=in_), reads=[in_], writes=[out])
        else:
            S.op(eng, lambda h: h.tensor_copy(out=out, in_=in_), reads=[in_], writes=[out])

    def MEMSET(eng, out, val):
        eng = _e(eng)
        S.op(eng, lambda h: h.memset(out, val), writes=[out])

    def MM(out, lhsT, rhs, start, stop):
        S.op("pe", lambda h: h.matmul(out, lhsT=lhsT, rhs=rhs, start=start, stop=stop), reads=[lhsT, rhs], writes=[out])

    def TR(out, in_):
        S.op("pe", lambda h: h.transpose(out, in_, ident[:]), reads=[in_, ident[:]], writes=[out])

    def DMA(eng, out, in_, slot, reads=(), writes=()):
        S.op(eng, lambda h: h.dma_start(out=out, in_=in_), reads=list(reads), writes=list(writes), dma_slot=slot)

    def dbg(name, ap):
        if not debug:
            return
        o = nc.dram_tensor("dbg_" + name, list(ap.shape), ap.dtype, kind="ExternalOutput").ap()
        dbg_outs[name] = o
        DMA("pool", o, ap, "dbg_" + name, reads=[ap])


    def ps_bank():
        b = st_["ps"] % 6
        st_["ps"] += 1
        return PS[:, b, :]

    def ps_pair():
        if st_["ps"] % 2:
            st_["ps"] += 1
        b = st_["ps"] % 6
        st_["ps"] += 2
        return PS[:, b:b + 2, :].rearrange("p a n -> p (a n)")

    def tr_bank():
        b = st_["tr"] % 2
        st_["tr"] += 1
        return PTR[:, b, :]

    def wk():
        b = st_["wk"] % NWORK
        st_["wk"] += 1
        return work[:, b, :]

    def wload(name, blk):
        slot = st_["ring"] % NW
        st_["ring"] += 1
        dst = ring[:, slot, :, :]
        if st_["gi0"]:
            DMA("pool", dst, src_ap[(name, blk)], "cring%d" % slot, writes=[dst])
            DMA("sp", scr[name][blk], dst, "rst%d" % slot, reads=[dst], writes=[("scr_" + name, blk, blk + 1)])
        else:
            DMA("sp", dst, scr[name][blk], "ring%d" % slot, reads=[("scr_" + name, blk, blk + 1)], writes=[dst])
        return dst

    def cload(dst, src, i):
        DMA("pool", dst, src, "c%d" % i, writes=[dst])

    def vcol(name, i):
        o = VO[name] + i
        return vecs[:, o:o + 1]

    def x_load(gi):
        xr = xres[gi % 2]
        for sub in range(4):
            r0 = gi * T + sub * 128
            DMA("pool", xr[:, sub, :], x_d[r0:r0 + 128, :], "x%d_%d" % (gi % 2, sub), writes=[xr[:, sub, :]])

    def pos_load(gi):
        si, ti = divmod(gi, ntile)
        DMA("pool", posi[:], pos_d[si:si + 1, ti * T:(ti + 1) * T].partition_broadcast(128), "pos", writes=[posi[:]])

    cload(vecs[:], vecs_d, 0)
    pos_load(0)
    x_load(0)
    cload(ident[:], ident_d, 1)
    cload(wb1[:], n1_d.partition_broadcast(128), 4)
    cload(Dp[:], dp_d, 2)
    cload(QD[:], qd_d, 3)
    cload(wb2[:], n2_d.partition_broadcast(128), 5)
    cload(wfb[:], nf_d.partition_broadcast(128), 6)
    lam = vecs[:, VO["lam"]:VO["lam"] + 8]
    ACT(ctmp[:], lam, AF.Exp, scale=-1.0)
    ACT(ctmp[:], ctmp[:], AF.Ln, bias=1.0)
    TS("dve", c1h[:], ctmp[:], -4.0, None, ALU.mult)
    TS("dve", hbias[:, 0:8], vecs[:, VO["lbr"]:VO["lbr"] + 8], 0.5, None, ALU.mult)
    TS("dve", hbias[:, 8:16], vecs[:, VO["lbi"]:VO["lbi"] + 8], 0.5, None, ALU.mult)
    TS("dve", hbias[:, 16:32], vecs[:, VO["mgb"]:VO["mgb"] + 16], 0.5, None, ALU.mult)

    def cast_blk(name, blk, ap):
        src_ap[(name, blk)] = ap

    for b in range(14):
        cast_blk("w_in", b, w_in_d[:, b * 512:(b + 1) * 512].rearrange("(kc p) n -> p kc n", p=128))
    DMA("pool", Wr[:], wr_d.rearrange("n (kk p) j -> p n kk j", p=128), "c7", writes=[Wr[:]])
    DMA("pool", Wi[:], wi_d.rearrange("n (kk p) j -> p n kk j", p=128), "c8", writes=[Wi[:]])
    for b in range(2):
        cast_blk("w_ro", b, w_ro_d[:, b * 512:(b + 1) * 512].rearrange("(kc p) n -> p kc n", p=128))
        cast_blk("w_lo", b, w_lo_d[:, b * 512:(b + 1) * 512].rearrange("(kc p) n -> p kc n", p=128))
    for b in range(2):
        cast_blk("w_o", b, w_o_d[:, b * 512:(b + 1) * 512].rearrange("(kc p) n -> p kc n", p=128))
    for g in range(6):
        cast_blk("w_up", g, w_up_d[:, g * 512:(g + 1) * 512].rearrange("(kc p) n -> p kc n", p=128))
        cast_blk("w_up", 6 + g, w_up_d[:, DFF + g * 512:DFF + (g + 1) * 512].rearrange("(kc p) n -> p kc n", p=128))
    for half in range(2):
        for kg in range(3):
            cast_blk("w_dn", half * 3 + kg,
                     w_dn_d[kg * 1024:(kg + 1) * 1024, half * 512:(half + 1) * 512].rearrange("(kc p) n -> p kc n", p=128))

    def norm_stats(xr):
        sqrt_warm()
        for sub in range(4):
            ACT(xn[:, sub % 2, :], xr[:, sub, :], AF.Square, accum=stat[:, sub:sub + 1])
        ACT(stat[:, 4:8], stat[:, 0:4], AF.Sqrt, scale=1.0 / D, bias=RMS_EPS)
        S.op("dve", lambda h: h.reciprocal(out=stat[:, 8:12], in_=stat[:, 4:8]), reads=[stat[:, 4:8]], writes=[stat[:, 8:12]])

    def norm_scale(xr, wb):
        def mk(sub):
            def f():
                xs = xn[:, sub % 2, :]
                STT(xs, xr[:, sub, :], stat[:, 8 + sub:9 + sub], wb[:], ALU.mult, ALU.mult)
                tb = tr_bank()
                for c in range(8):
                    TR(tb[:, c * 128:(c + 1) * 128], xs[:, c * 128:(c + 1) * 128])
                COPY("act", hT[:, :, sub * 128:(sub + 1) * 128], tb.rearrange("p (c n) -> p c n", n=128))
            return f
        return [mk(sub) for sub in range(4)]

    def sqrt_warm():
        ACT(stat[:, 23:24], vcol("sgn", 0), AF.Sqrt, scale=0.0, bias=1.0)

    def tables(gi):
        ang = wk()
        ki = wk().bitcast(I32)
        TS("dve", ang, posi[:], vcol("invf", 0), None, ALU.mult)
        TS("dve", ki, ang, 1.0 / TWO_PI, None, ALU.mult)
        r1 = wk()
        STT(r1, ki, -CW1, ang, ALU.mult, ALU.add)
        STT(r1, ki, -CW2, r1, ALU.mult, ALU.add)
        m = wk()
        TS("dve", m, r1, float(np.pi), -TWO_PI, ALU.is_gt, ALU.mult)
        TT("dve", r1, r1, m, ALU.add)
        TS("dve", m, r1, -float(np.pi), TWO_PI, ALU.is_lt, ALU.mult)
        TT("dve", r1, r1, m, ALU.add)
        ACT(sin_t[:], r1, AF.Sin, scale=vcol("sgn", 0))
        cc = wk()
        TS("dve", cc, r1, float(np.pi / 2), None, ALU.add)
        TS("dve", m, cc, float(np.pi), -TWO_PI, ALU.is_gt, ALU.mult)
        TT("dve", cc, cc, m, ALU.add)
        ACT(cos_t[:], cc, AF.Sin)
        if gi + 1 < nseq * ntile:
            pos_load(gi + 1)

    def pre_phase(gi, spread=False):
        tables(gi)
        norm_stats(xres[gi % 2])
        subs = norm_scale(xres[gi % 2], wb1)
        if not spread:
            for f in subs:
                f()
            return []
        subs[0]()
        return subs[1:]

    def proj_fm(wblk, j):
        ps = ps_bank()
        for kc in range(8):
            MM(ps, wblk[:, kc, j * 128:(j + 1) * 128], hT[:, kc, :], kc == 0, kc == 7)
        return ps

    def rotary(ps, dst, qdh=None):
        qs = wk()
        COPY("act", qs, ps)
        t1 = wk()
        t2 = wk()
        TT("dve", t1, qs, cos_t[:], ALU.mult)
        TT("dve", t2[0:64, :], qs[64:128, :], sin_t[64:128, :], ALU.mult)
        TT("dve", t2[64:128, :], qs[0:64, :], sin_t[0:64, :], ALU.mult)
        TT("dve", dst, t1, t2, ALU.add)

    lcw = lambda k, c: vcol("lcw", k * 8 + c)
    fcw = lambda k, c: vcol("fcw", k * NCH_FF + c)

    def main_phase(gi):
        si, ti = divmod(gi, ntile)
        first = (ti == 0)
        xr_all = xres[gi % 2]
        if gi + 1 < nseq * ntile:
            x_load(gi + 1)
        if first:
            MEMSET("pool", Sf[:], 0.0)
            MEMSET("pool", Sbf[:, :, 0, :], 0.0)
            MEMSET("pool", halo_l[:], 0.0)
            MEMSET("pool", halo_f[:], 0.0)
        if debug and gi == 0:
            dbg("cos", cos_t[:])
            dbg("sin", sin_t[:])
            dbg("h1T", hT[:])
        def v_half(half):
            wv = wload("w_in", 2 + half)
            for sub in range(4):
                ps = ps_bank()
                for kc in range(8):
                    MM(ps, hT[:, kc, sub * 128:(sub + 1) * 128], wv[:, kc, :], kc == 0, kc == 7)
                COPY("act", v_tm[:, sub, half * 512:(half + 1) * 512], ps)

        wq = wload("w_in", 0)
        for h in range(HEADS):
            ps = proj_fm(wq, h)
            rotary(ps, qd[:, h, :], QD[:, h, :])
        v_half(0)
        wk_ = wload("w_in", 1)
        for h in range(HEADS):
            ps = proj_fm(wk_, h)
            rotary(ps, krot[:, h, :])
        v_half(1)
        def xlru_half(half):
            wx = wload("w_in", 6 + half)
            todo = []
            for j in range(4):
                c = half * 4 + j
                ps = proj_fm(wx, j)
                xs = rawb[:, c % 4, :]
                COPY("act", xs[:, 3:T + 3], ps)
                COPY("pool", xs[:, 0:3], halo_l[:, c, :])
                todo.append((c, xs))

            def conv():
                for c, xs in todo:
                    if half == 1:
                        acc = wk()
                        tmp = wk()
                        TS("pool", acc, xs[:, 3:T + 3], lcw(3, c), vcol("lcb", c), ALU.mult, ALU.add)
                        for k in (2, 1, 0):
                            TS("pool", tmp, xs[:, k:T + k], lcw(k, c), 0.0, ALU.mult, ALU.add)
                            TT("pool", xcb[:, c, :] if k == 0 else acc, acc, tmp, ALU.add)
                        COPY("pool", halo_l[:, c, :], xs[:, T:T + 3])
                        continue
                    acc = wk()
                    TS("dve", acc, xs[:, 3:T + 3], lcw(3, c), vcol("lcb", c), ALU.mult, ALU.add)
                    STT(acc, xs[:, 2:T + 2], lcw(2, c), acc, ALU.mult, ALU.add)
                    STT(acc, xs[:, 1:T + 1], lcw(1, c), acc, ALU.mult, ALU.add)
                    STT(xcb[:, c, :], xs[:, 0:T], lcw(0, c), acc, ALU.mult, ALU.add)
                    COPY("pool", halo_l[:, c, :], xs[:, T:T + 3])
            return conv

        for hp in range(2):
            tb = tr_bank()
            for hh in range(2):
                h = hp * 2 + hh
                for b in range(NB):
                    TR(tb[:, (hh * NB + b) * 128:(hh * NB + b + 1) * 128], krot[:, h, b * 128:(b + 1) * 128])
            for hh in range(2):
                h = hp * 2 + hh
                ACT(kd[:, h, :, :].rearrange("p b d -> p (b d)"), tb[:, hh * 512:(hh + 1) * 512], AF.Copy,
                    scale=vcol("kdec", h))
        for h in range(HEADS):
            ps = ps_bank()
            for b in range(NB):
                MM(ps[:, b * 128:(b + 1) * 128], krot[:, h, b * 128:(b + 1) * 128], qd[:, h, b * 128:(b + 1) * 128], True, True)
            TT("dve", PTb[:, h, :].rearrange("p (b i) -> p b i", i=128), ps.rearrange("p (b i) -> p b i", i=128),
               Dp[:, h, :].unsqueeze(1).to_broadcast([128, NB, 128]), ALU.mult)
        xlru_half(0)()
        conv1 = xlru_half(1)

        def gret_half(half):
            wg = wload("w_in", 4 + half)
            for j in range(4):
                c = half * 4 + j
                ps = proj_fm(wg, j)
                ACT(sg[:, c, :], ps, AF.Silu)
                TS("dve", sg[:, c, :], sg[:, c, :], vcol("gnw", c), None, ALU.mult)

        upairs = []
        for h in range(HEADS):
            pp = ps_pair()
            for b in range(NB):
                MM(pp[:, b * DV:(b + 1) * DV], kd[:, h, b, :], v_tm[:, b, h * DV:(h + 1) * DV], True, True)
            upairs.append(pp)
            if h % 2 == 1:
                for b in range(NB):
                    for h2 in (h - 1, h):
                        u_ = upairs[h2][:, b * DV:(b + 1) * DV]
                        STT(Sbf[:, h2, b + 1, :], Sf[:, h2, :], cdec[h2], u_, ALU.mult, ALU.add)
                        STT(Sf[:, h2, :], Sf[:, h2, :], cdec[h2], u_, ALU.mult, ALU.add)
                gret_half(h // 2)
        if debug and gi == 0:
            dbg("qd", qd[:])
            dbg("krot", krot[:])
            dbg("v", v_tm)
            dbg("xcb", xcb)

        wy = [None, None]

        def o_mm(b):
            pp = ps_pair()
            for h in range(HEADS):
                o_ = pp[:, h * DV:(h + 1) * DV]
                MM(o_, PTb[:, h, b * 128:(b + 1) * 128], v_tm[:, b, h * DV:(h + 1) * DV], True, False)
                MM(o_, qd[:, h, b * 128:(b + 1) * 128], Sbf[:, h, b, :], False, True)
            return pp

        def o_norm(b, pp):
            for h in range(HEADS):
                S.op("dve", lambda hd, h=h, pp=pp: hd.bn_stats(out=st6[:, h, :], in_=pp[:, h * DV:(h + 1) * DV]),
                     reads=[pp[:, h * DV:(h + 1) * DV]], writes=[st6[:, h, :]])
            for h in range(HEADS):
                S.op("dve", lambda hd, h=h: hd.bn_aggr(out=mv[:, h, :], in_=st6[:, h, :]),
                     reads=[st6[:, h, :]], writes=[mv[:, h, :]])
            TT("dve", gstat[:, 0, :], mv[:, :, 1], vecs[:, VO["epsq"]:VO["epsq"] + HEADS], ALU.add)
            ACT(gstat[:, 0, :], gstat[:, 0, :], AF.Sqrt)
            S.op("dve", lambda hd: hd.reciprocal(out=gstat[:, 1, :], in_=gstat[:, 0, :]), reads=[gstat[:, 0, :]], writes=[gstat[:, 1, :]])
            STT(gstat[:, 2, :], mv[:, :, 0], -1.0, gstat[:, 1, :], ALU.mult, ALU.mult)
            ob = xn[:, b % 2, :]
            for h in range(HEADS):
                ACT(ob[:, h * DV:(h + 1) * DV], pp[:, h * DV:(h + 1) * DV], AF.Identity,
                    bias=gstat[:, 2, h:h + 1], scale=gstat[:, 1, h:h + 1])
            if debug and gi == 0 and b == 0:
                dbg("onb0", ob)

        def o_tr(b):
            ob = xn[:, b % 2, :]
            tb = tr_bank()
            for c in range(8):
                TR(tb[:, c * 128:(c + 1) * 128], ob[:, c * 128:(c + 1) * 128])
            og_b = sg[:, :, b * 128:(b + 1) * 128]
            TT("dve", og_b, tb.rearrange("p (c n) -> p c n", n=128), og_b, ALU.mult)

        def ylru(half):
            wyb = wload("w_in", 8 + half)
            for j in range(4):
                c = half * 4 + j
                ps = proj_fm(wyb, j)
                ACT(gy[:, c, :], ps, AF.Gelu_apprx_tanh)

        pp0 = o_mm(0)
        o_norm(0, pp0)
        pp1 = o_mm(1)
        o_norm(1, pp1)
        conv1()
        ylru(0)
        o_tr(0)
        pp2 = o_mm(2)
        o_norm(2, pp2)
        o_tr(1)
        pp3 = o_mm(3)
        o_norm(3, pp3)
        ylru(1)
        o_tr(2)
        o_tr(3)
        if not (ti == ntile - 1):
            COPY("pool", Sbf[:, :, 0, :], Sbf[:, :, NB, :])

        mg_list = [(which, half) for which in range(2) for half in range(2)]

        def merge_gate_block(which, half):
            dst = gr if which == 0 else gl
            wg = wload("w_in", 10 + which * 2 + half)
            for j in range(4):
                c = half * 4 + j
                ps = proj_fm(wg, j)
                ACT(dst[:, c, :], ps, AF.Tanh, bias=hbias[:, 16 + which * 8 + c:17 + which * 8 + c], scale=0.5)
                TS("pool", dst[:, c, :], dst[:, c, :], 0.5, 0.5, ALU.mult, ALU.add)

        for cp in range(4):
            if cp < 3:
                merge_gate_block(*mg_list[cp])
            bufs = []
            for c in (2 * cp, 2 * cp + 1):
                n = c // 2
                cc_ = c % 2
                psr = ps_bank()
                for kk in range(2):
                    MM(psr, Wr[:, n, kk, cc_ * 128:(cc_ + 1) * 128], xcb[:, 2 * n + kk, :], kk == 0, kk == 1)
                psi = ps_bank()
                for kk in range(2):
                    MM(psi, Wi[:, n, kk, cc_ * 128:(cc_ + 1) * 128], xcb[:, 2 * n + kk, :], kk == 0, kk == 1)
                r_ = wk()
                i_ = wk()
                a_ = wk()
                ACT(r_, psr, AF.Tanh, bias=hbias[:, c:c + 1], scale=0.5)
                ACT(i_, psi, AF.Tanh, bias=hbias[:, 8 + c:9 + c], scale=0.5)
                ACT(a_, r_, AF.Exp, bias=c1h[:, c:c + 1], scale=c1h[:, c:c + 1])
                TT("dve", r_, a_, a_, ALU.mult)
                bufs.append((c, r_, i_, a_))
            for (c, r_, i_, a_) in bufs:
                ACT(r_, r_, AF.Sqrt, scale=-1.0, bias=1.0)
            for (c, r_, i_, a_) in bufs:
                STT(i_, i_, 1.0, r_, ALU.add, ALU.mult)
                STT(i_, i_, 0.5, xcb[:, c, :], ALU.mult, ALU.mult)
                hl = wk()
                init = 0.0 if first else hstate[:, c:c + 1]
                S.op("dve", lambda hd, hl=hl, a_=a_, i_=i_, init=init: hd.tensor_tensor_scan(
                    out=hl, data0=a_, data1=i_, initial=init, op0=ALU.mult, op1=ALU.add),
                    reads=aps(a_, i_, init), writes=[hl])
                COPY("pool", hstate[:, c:c + 1], hl[:, T - 1:T])
                if debug and gi == 0 and c == 0:
                    dbg("hl0", hl)
                TT("dve", gy[:, c, :], hl, gy[:, c, :], ALU.mult)
        merge_gate_block(*mg_list[3])
        sqrt_warm()
        if debug and gi == 0:
            dbg("og", sg)
            dbg("pg", gy)
        for half in range(2):
            wro = wload("w_ro", half)
            wlo = wload("w_lo", half)
            m1s = []
            for j in range(4):
                oc = half * 4 + j
                psa = ps_bank()
                for kc in range(8):
                    MM(psa, wro[:, kc, j * 128:(j + 1) * 128], sg[:, kc, :], kc == 0, kc == 7)
                m1 = wk()
                TT("dve", m1, psa, gr[:, oc, :], ALU.mult)
                m1s.append(m1)
            for j in range(4):
                oc = half * 4 + j
                psb = ps_bank()
                for kc in range(8):
                    MM(psb, wlo[:, kc, j * 128:(j + 1) * 128], gy[:, kc, :], kc == 0, kc == 7)
                m2 = wk()
                TT("dve", m2, psb, gl[:, oc, :], ALU.mult)
                TT("pool", mixT[:, oc, :], m1s[j], m2, ALU.add)
        if debug and gi == 0:
            dbg("mixT", mixT)
        wo = [wload("w_o", 0), wload("w_o", 1)]

        def n2_slot(sub):
            if sub < 2:
                return xn[:, sub, :]
            return gl[:, 4 + (sub - 2) * 2:6 + (sub - 2) * 2, :].rearrange("p a n -> p (a n)")

        def n2_tr(sub):
            xs = n2_slot(sub)
            tb = tr_bank()
            for c in range(8):
                TR(tb[:, c * 128:(c + 1) * 128], xs[:, c * 128:(c + 1) * 128])
            COPY("act", hT[:, :, sub * 128:(sub + 1) * 128], tb.rearrange("p (c n) -> p c n", n=128))

        def n2_stt(sub):
            STT(n2_slot(sub), xr_all[:, sub, :], stat[:, 8 + sub:9 + sub], wb2[:], ALU.mult, ALU.mult)

        for sub in range(4):
            for half in range(2):
                ps = ps_bank()
                for kc in range(8):
                    MM(ps, mixT[:, kc, sub * 128:(sub + 1) * 128], wo[half][:, kc, :], kc == 0, kc == 7)
                xr = xr_all[:, sub, half * 512:(half + 1) * 512]
                TT("dve", xr, xr, ps, ALU.add)
            if sub < 3:
                ACT(gl[:, (sub % 2) * 2:(sub % 2) * 2 + 2, :].rearrange("p a n -> p (a n)"), xr_all[:, sub, :], AF.Square,
                    accum=stat[:, sub:sub + 1])
            if sub == 2:
                ACT(stat[:, 4:7], stat[:, 0:3], AF.Sqrt, scale=1.0 / D, bias=RMS_EPS)
                S.op("dve", lambda h: h.reciprocal(out=stat[:, 8:11], in_=stat[:, 4:7]), reads=[stat[:, 4:7]], writes=[stat[:, 8:11]])
                n2_stt(0)
                n2_stt(1)
        if debug and gi == 0:
            dbg("x1", xr_all[:])
        n2_stt(2)
        n2_tr(0)
        n2_tr(1)
        ACT(gl[:, 2:4, :].rearrange("p a n -> p (a n)"), xr_all[:, 3, :], AF.Square, accum=stat[:, 3:4])
        ACT(stat[:, 7:8], stat[:, 3:4], AF.Sqrt, scale=1.0 / D, bias=RMS_EPS)
        S.op("dve", lambda h: h.reciprocal(out=stat[:, 11:12], in_=stat[:, 7:8]), reads=[stat[:, 7:8]], writes=[stat[:, 11:12]])
        n2_stt(3)
        n2_tr(2)
        n2_tr(3)
        for g in range(6):
            wgt = wload("w_up", g)
            wvl = wload("w_up", 6 + g)
            for j in range(4):
                oc = g * 4 + j
                psg = proj_fm(wgt, j)
                psv = proj_fm(wvl, j)
                gs = rawb[:, oc % 4, :]
                COPY("act", gs[:, 2:T + 2], psg)
                COPY("pool", gs[:, 0:2], halo_f[:, oc, :])
                acc = wk()
                TS("dve", acc, gs[:, 2:T + 2], fcw(2, oc), vcol("fcb", oc), ALU.mult, ALU.add)
                STT(acc, gs[:, 1:T + 1], fcw(1, oc), acc, ALU.mult, ALU.add)
                STT(acc, gs[:, 0:T], fcw(0, oc), acc, ALU.mult, ALU.add)
                COPY("pool", halo_f[:, oc, :], gs[:, T:T + 2])
                ACT(acc, acc, AF.Gelu_apprx_tanh)
                TT("dve", pf[:, oc, :], acc, psv, ALU.mult)
        if debug and gi == 0:
            dbg("pf", pf)

    def down_phase(gi, mid_hook=None):
        xr_all = xres[gi % 2]
        pending = []
        for half in range(2):
            pss = [ps_bank() for _ in range(4)]
            for kg in range(3):
                wd = wload("w_dn", half * 3 + kg)
                for pair in ((0, 1), (2, 3)):
                    for kc in range(8):
                        c = kg * 8 + kc
                        for sub in pair:
                            MM(pss[sub], pf[:, c, sub * 128:(sub + 1) * 128], wd[:, kc, :], c == 0, c == NCH_FF - 1)
                    if half == 1 and pending:
                        pending.pop(0)()
                if half == 0 and kg == 2 and mid_hook is not None:
                    pending = mid_hook()
            for sub in range(4):
                xr = xr_all[:, sub, half * 512:(half + 1) * 512]
                TT("dve", xr, xr, pss[sub], ALU.add)
        sqrt_warm()
        for sub in range(4):
            ACT(xn[:, sub % 2, :], xr_all[:, sub, :], AF.Square, accum=stat[:, 12 + sub:13 + sub])
        ACT(stat[:, 16:20], stat[:, 12:16], AF.Sqrt, scale=1.0 / D, bias=RMS_EPS)
        S.op("dve", lambda h: h.reciprocal(out=stat[:, 20:24], in_=stat[:, 16:20]), reads=[stat[:, 16:20]], writes=[stat[:, 20:24]])
        for sub in range(4):
            STT(xr_all[:, sub, :], xr_all[:, sub, :], stat[:, 20 + sub:21 + sub], wfb[:], ALU.mult, ALU.mult)
            r0 = gi * T + sub * 128
            DMA("pool", out_d[r0:r0 + 128, :], xr_all[:, sub, :], "o%d_%d" % (gi % 2, sub), reads=[xr_all[:, sub, :]])

    ntot = nseq * ntile
    pre_phase(0)
    for gi in range(ntot):
        st_["gi0"] = (gi == 0)
        main_phase(gi)
        if gi + 1 < ntot:
            down_phase(gi, mid_hook=lambda gi=gi: pre_phase(gi + 1, spread=True))
        else:
            down_phase(gi)

    counts = S.emit()
    st.close()
    return nc, counts, dbg_outs


def prep_inputs(inputs, core, nseq=4, ntile=4):
    f = lambda a: np.ascontiguousarray(np.asarray(a), dtype=np.float32)
    dpc, qdc, kdecc, cdec, invfc, sgnc, epsqc = host_consts()
    x = f(inputs["x"])
    ntok = nseq * ntile * T
    xs = x[core * nseq:(core + 1) * nseq, :ntile * T, :].reshape(ntok, D)
    pos = np.ascontiguousarray(np.asarray(inputs["positions"])[core * nseq:(core + 1) * nseq, :ntile * T].astype(np.int32))
    fm = lambda v, nch: f(v).reshape(nch, 128).T
    cols = {
        "mgb": np.concatenate([fm(inputs["merge_gate_b"][0, 0], 8), fm(inputs["merge_gate_b"][0, 1], 8)], axis=1),
        "gnw": fm(inputs["ret_gn_w"][0], 8),
        "lcw": np.concatenate([fm(inputs["lru_conv_w"][0, k], 8) for k in range(4)], axis=1),
        "lcb": fm(inputs["lru_conv_b"][0], 8),
        "lbr": fm(np.asarray(inputs["lru_b_r"][0]).reshape(-1), 8),
        "lbi": fm(np.asarray(inputs["lru_b_i"][0]).reshape(-1), 8),
        "lam": fm(inputs["lru_lambda"][0], 8),
        "fcw": np.concatenate([fm(inputs["ffn_conv_w"][0, k], NCH_FF) for k in range(3)], axis=1),
        "fcb": fm(inputs["ffn_conv_b"][0], NCH_FF),
        "kdec": kdecc,
        "invf": invfc[:, None],
        "sgn": sgnc[:, None],
        "epsq": epsqc,
    }
    vecs = np.zeros((128, NV), np.float32)
    for k, o in VO.items():
        a = cols[k]
        vecs[:, o:o + a.shape[1]] = a
    return {
        "x": np.ascontiguousarray(xs),
        "pos": pos,
        "w_in": f(inputs["w_in"][0]),
        "w_ret_o": f(inputs["w_ret_o"][0]),
        "w_lru_o": f(inputs["w_lru_o"][0]),
        "w_out": f(inputs["w_out"][0]),
        "ffn_w_up": f(inputs["ffn_w_up"][0]),
        "ffn_w_down": f(inputs["ffn_w_down"][0]),
        "lru_w_r": f(inputs["lru_w_r"][0]),
        "lru_w_i": f(inputs["lru_w_i"][0]),
        "norm1_w": f(inputs["norm1_w"][0])[None, :],
        "norm2_w": f(inputs["norm2_w"][0])[None, :],
        "norm_f_w": f(inputs["norm_f_w"]).reshape(1, D),
        "vecs": vecs,
        "dpc": dpc,
        "qdc": qdc,
        "identc": np.eye(128, dtype=np.float32),
    }


_CACHE = {}


def kernel(**inputs):
    ncores = 8
    if "nc" not in _CACHE:
        _CACHE["nc"] = build(4, 4)[0]
    nc = _CACHE["nc"]
    in_maps = [prep_inputs(inputs, c) for c in range(ncores)]
    res = run_bass_kernel_spmd(nc, in_maps, core_ids=list(range(ncores)))
    outs = [np.asarray(r["out"], dtype=np.float32).reshape(4, SEQ, D) for r in res.results]
    return np.concatenate(outs, axis=0)
```

```python
import numpy as np
import ml_dtypes
from contextlib import ExitStack
import concourse.bass as bass
import concourse.mybir as mybir
from concourse.bass_utils import run_bass_kernel_spmd

F32 = mybir.dt.float32
BF16 = mybir.dt.bfloat16
I32 = mybir.dt.int32
AF = mybir.ActivationFunctionType
ALU = mybir.AluOpType
_DT_SIZE = {F32: 4, BF16: 2, I32: 4}
ENGS = ("pe", "act", "dve", "pool", "sp")


def ap_region(ap):
    es = _DT_SIZE[ap.dtype]
    pairs = ap.ap
    pstep = pairs[0][0]
    off = ap.offset
    lo = off % pstep if pstep > 0 else off
    span = 0
    for st, cnt in pairs[1:]:
        span += abs(st) * (cnt - 1)
    lo_b, hi_b = lo * es, (lo + span + 1) * es
    if ap.tensor.name in ("PS", "PTR"):
        lo_b = (lo_b // 2048) * 2048
        hi_b = ((hi_b + 2047) // 2048) * 2048
    return (ap.tensor.name, lo_b, hi_b)


class Op:
    __slots__ = ("eng", "fn", "seq", "waits", "token", "dma_sem")

    def __init__(self, eng, fn):
        self.eng = eng
        self.fn = fn
        self.waits = {}
        self.token = None
        self.dma_sem = None


class Sched:
    def __init__(self, nc):
        self.nc = nc
        self.ops = {e: [] for e in ENGS}
        self.track = {}
        self.dma_cnt = {}

    def _segs(self, name, lo, hi):
        lst = self.track.get(name)
        if lst is None:
            lst = []
        out = []
        new = []
        cur = lo
        lst.sort(key=lambda r: r[0])
        for r in lst:
            if r[1] <= lo or r[0] >= hi:
                new.append(r)
                continue
            if r[0] < lo:
                new.append([r[0], lo, r[2], dict(r[3])])
                r = [lo, r[1], r[2], r[3]]
            if r[1] > hi:
                new.append([hi, r[1], r[2], dict(r[3])])
                r = [r[0], hi, r[2], r[3]]
            if r[0] > cur:
                g = [cur, r[0], None, {}]
                new.append(g)
                out.append(g)
            new.append(r)
            out.append(r)
            cur = r[1]
        if cur < hi:
            g = [cur, hi, None, {}]
            new.append(g)
            out.append(g)
        self.track[name] = new
        return out

    @staticmethod
    def _add_wait(op, tok):
        if tok is None:
            return
        s, v = tok
        if op.waits.get(s, 0) < v:
            op.waits[s] = v

    def op(self, eng, fn, reads=(), writes=(), dma_slot=None):
        o = Op(eng, fn)
        lst = self.ops[eng]
        o.seq = len(lst) + 1
        if dma_slot is not None:
            c = self.dma_cnt.get(dma_slot, 0) + 16
            self.dma_cnt[dma_slot] = c
            o.dma_sem = dma_slot
            o.token = ("dma:" + dma_slot, c)
        else:
            o.token = (eng, o.seq)
        s, v = o.token
        rregs = [r if isinstance(r, tuple) else ap_region(r) for r in reads]
        wregs = [w if isinstance(w, tuple) else ap_region(w) for w in writes]
        for reg in rregs:
            for seg in self._segs(*reg):
                self._add_wait(o, seg[2])
        for reg in wregs:
            for seg in self._segs(*reg):
                self._add_wait(o, seg[2])
                for s2, v2 in seg[3].items():
                    self._add_wait(o, (s2, v2))
        for reg in rregs:
            for seg in self._segs(*reg):
                if seg[3].get(s, 0) < v:
                    seg[3][s] = v
        for reg in wregs:
            for seg in self._segs(*reg):
                seg[2] = o.token
                seg[3] = {}
        lst.append(o)
        return o

    def emit(self, final_waits_eng="sp", same_engine_sync=("act", "dve", "pool")):
        nc = self.nc
        need = {e: set() for e in ENGS}
        for e in ENGS:
            for o in self.ops[e]:
                for s, v in list(o.waits.items()):
                    if s.startswith("dma:"):
                        if s == "dma:" + str(o.dma_sem) and v >= o.token[1]:
                            del o.waits[s]
                        continue
                    if s == e and (e not in same_engine_sync or v >= o.seq):
                        del o.waits[s]
                        continue
                    need[s].add(v)
        rank = {}
        for e in ENGS:
            for i, v in enumerate(sorted(need[e])):
                rank[(e, v)] = i + 1
        stack = ExitStack()
        sems = {}
        for e in ENGS:
            sems[e] = stack.enter_context(nc.semaphore("s_" + e))
        for slot in self.dma_cnt:
            sems["dma:" + slot] = stack.enter_context(nc.semaphore("d_" + slot))
        sched = self

        def run_engine(ename, handle):
            waited = {}
            for o in sched.ops[ename]:
                for s, v in o.waits.items():
                    val = v if s.startswith("dma:") else rank[(s, v)]
                    if waited.get(s, 0) >= val:
                        continue
                    handle.wait_ge(sems[s], val)
                    waited[s] = val
                ins = o.fn(handle)
                if o.dma_sem is not None:
                    ins.then_inc(sems["dma:" + o.dma_sem], 16)
                elif (ename, o.seq) in rank:
                    ins.then_inc(sems[ename], 1)
            if ename == final_waits_eng:
                for slot, c in sched.dma_cnt.items():
                    if waited.get("dma:" + slot, 0) < c:
                        handle.wait_ge(sems["dma:" + slot], c)

        block = stack.enter_context(nc.Block())

        @block.tensor
        def _(h):
            run_engine("pe", h)

        @block.scalar
        def _(h):
            run_engine("act", h)

        @block.vector
        def _(h):
            run_engine("dve", h)

        @block.gpsimd
        def _(h):
            run_engine("pool", h)

        @block.sync
        def _(h):
            run_engine("sp", h)

        stack.close()
        return {e: len(self.ops[e]) for e in ENGS}


D = 1024
SEQ = 2048
T = 512
NB = T // 128
HEADS = 4
DK = 128
DV = 256
DFF = 3072
NCH_FF = DFF // 128
RMS_EPS = 1e-6
GN_EPS = 1e-6
TWO_PI = 2.0 * np.pi
CW1 = 6.28125
CW2 = float(TWO_PI - 6.28125)

VO = {}
_c = 0
for _n, _w in [("mgb", 16), ("gnw", 8), ("lcw", 32), ("lcb", 8), ("lbr", 8), ("lbi", 8),
               ("lam", 8), ("fcw", 72), ("fcb", 24), ("kdec", 4), ("invf", 1), ("sgn", 1), ("epsq", 4)]:
    VO[_n] = _c
    _c += _w
NV = _c

W_BLOCKS = [("w_in", 14), ("w_ro", 2), ("w_lo", 2), ("w_o", 2), ("w_up", 12), ("w_dn", 6)]


def host_consts():
    lg = np.log1p(-np.power(2.0, -5.0 - np.arange(HEADS, dtype=np.float64)))
    s = DK ** -0.5
    idx = np.arange(128, dtype=np.float64)
    j = idx[:, None]
    i = idx[None, :]
    mask = ((j // 64) <= (i // 64)).astype(np.float64)
    dp = np.zeros((128, HEADS, 128), np.float32)
    qd = np.zeros((128, HEADS, 128), np.float32)
    kdec = np.zeros((128, HEADS), np.float32)
    for h in range(HEADS):
        dp[:, h, :] = (s * np.exp(lg[h] * (np.abs(i - j) - (i + 1.0))) * mask).astype(np.float32)
        qd[:, h, :] = np.exp(lg[h] * (idx + 1.0))[None, :].astype(np.float32)
        kdec[:, h] = (s * np.exp(lg[h] * (127.0 - idx))).astype(np.float32)
    cdec = [float(np.exp(lg[h] * 128.0)) for h in range(HEADS)]
    invf = np.power(np.float32(10000.0), -(np.arange(64, dtype=np.float32) / np.float32(64.0))).astype(np.float32)
    invf = np.concatenate([invf, invf])
    sgn = np.concatenate([np.ones(64, np.float32), -np.ones(64, np.float32)])
    epsq = np.zeros((128, HEADS), np.float32)
    for h in range(HEADS):
        epsq[:, h] = (GN_EPS / np.exp(2.0 * lg[h] * (idx + 1.0))).astype(np.float32)
    return dp, qd, kdec, cdec, invf, sgn, epsq


def build(nseq=4, ntile=4, debug=False):
    nc = bass.Bass("TRN2", target_bir_lowering=False)
    NTOK = nseq * ntile * T
    dpc, qdc, kdecc, cdec, invfc, sgnc, epsqc = host_consts()

    def din(name, shape, dt=F32):
        return nc.dram_tensor(name, shape, dt, kind="ExternalInput").ap()

    x_d = din("x", [NTOK, D])
    pos_d = din("pos", [nseq, ntile * T], I32)
    w_in_d = din("w_in", [D, 7168])
    w_ro_d = din("w_ret_o", [D, D])
    w_lo_d = din("w_lru_o", [D, D])
    w_o_d = din("w_out", [D, D])
    w_up_d = din("ffn_w_up", [D, 2 * DFF])
    w_dn_d = din("ffn_w_down", [DFF, D])
    wr_d = din("lru_w_r", [4, 256, 256])
    wi_d = din("lru_w_i", [4, 256, 256])
    n1_d = din("norm1_w", [1, D])
    n2_d = din("norm2_w", [1, D])
    nf_d = din("norm_f_w", [1, D])
    vecs_d = din("vecs", [128, NV])
    dp_d = din("dpc", [128, HEADS, 128])
    qd_d = din("qdc", [128, HEADS, 128])
    ident_d = din("identc", [128, 128])
    out_d = nc.dram_tensor("out", [NTOK, D], F32, kind="ExternalOutput").ap()

    scr = {}
    for name, nb in W_BLOCKS:
        scr[name] = nc.dram_tensor("scr_" + name, [nb, 128, 8, 512], BF16, kind="Internal").ap()

    S = Sched(nc)
    st = ExitStack()
    dbg_outs = {}

    def sb(name, shape, dt):
        return st.enter_context(nc.sbuf_tensor("sb_" + name, shape, dt))

    xres = [sb("xresA", [128, 4, D], F32), sb("xresB", [128, 4, D], F32)]
    hT = sb("hT", [128, 8, T], BF16)
    BIG = sb("BIG", [128, 24, T], BF16)
    B2 = sb("B2", [128, 16, T], BF16)
    qd = sb("qd", [128, HEADS, T], BF16)
    krot = sb("krot", [128, HEADS, T], BF16)
    kd = sb("kd", [128, HEADS, NB, 128], BF16)
    PTb = sb("PTb", [128, HEADS, T], BF16)
    Sf = sb("Sf", [128, HEADS, DV], F32)
    Sbf = sb("Sbf", [128, HEADS, NB + 1, DV], BF16)
    cos_t = sb("cos_t", [128, T], F32)
    sin_t = sb("sin_t", [128, T], F32)
    NWORK = 12
    work = sb("work", [128, NWORK, T], F32)
    xn = sb("xn", [128, 2, D], BF16)
    rawb = sb("rawb", [128, 4, T + 3], F32)
    halo_l = sb("halo_l", [128, 8, 3], F32)
    hstate = sb("hstate", [128, 8], F32)
    halo_f = sb("halo_f", [128, NCH_FF, 2], F32)
    Wr = sb("Wr", [128, 4, 2, 256], BF16)
    Wi = sb("Wi", [128, 4, 2, 256], BF16)
    ident = sb("ident", [128, 128], BF16)
    Dp = sb("Dp", [128, HEADS, 128], F32)
    QD = sb("QD", [128, HEADS, 128], F32)
    vecs = sb("vecs", [128, NV], F32)
    wb1 = sb("wb1", [128, D], BF16)
    wb2 = sb("wb2", [128, D], BF16)
    wfb = sb("wfb", [128, D], F32)
    c1h = sb("c1h", [128, 8], F32)
    hbias = sb("hbias", [128, 32], F32)
    ctmp = sb("ctmp", [128, 8], F32)
    stat = sb("stat", [128, 24], F32)
    st6 = sb("st6", [128, HEADS, 6], F32)
    mv = sb("mv", [128, HEADS, 2], F32)
    gstat = sb("gstat", [128, 3, HEADS], F32)
    posi = sb("posi", [128, T], I32)
    NW = 4
    ring = sb("ring", [128, NW, 8, T], BF16)

    print('sbuf remaining', nc.sbuf_bytes_remaining)
    PS = st.enter_context(nc.psum_tensor("PS", [128, 6, 512], F32))
    PTR = st.enter_context(nc.psum_tensor("PTR", [128, 2, 1024], BF16))

    sg = BIG[:, 0:8, :]
    v_tm = BIG[:, 8:16, :].rearrange("p (s a) n -> p s (a n)", a=2)
    gr = BIG[:, 8:16, :]
    xcb = BIG[:, 16:24, :]
    mixT = BIG[:, 16:24, :]
    pf = BIG[:]
    gl = B2[:, 0:8, :]
    gy = B2[:, 8:16, :]

    st_ = {"ps": 0, "tr": 0, "wk": 0, "ring": 0, "cast": 0, "gi0": False}
    src_ap = {}

    def aps(*xs):
        return [a for a in xs if not isinstance(a, (int, float)) and a is not None]

    def ACT(out, in_, func, bias=None, scale=None, accum=None):
        kw = {}
        if bias is not None:
            kw["bias"] = bias
        if scale is not None:
            kw["scale"] = scale
        if accum is not None:
            kw["accum_out"] = accum
        S.op("act", lambda h: h.activation(out=out, in_=in_, func=func, **kw),
             reads=aps(in_, bias, scale), writes=aps(out, accum))

    def _e(eng):
        return "dve" if (eng == "pool" and st_.get("gi0")) else eng

    def TT(eng, out, in0, in1, op):
        eng = _e(eng)
        S.op(eng, lambda h: h.tensor_tensor(out=out, in0=in0, in1=in1, op=op), reads=[in0, in1], writes=[out])

    def TS(eng, out, in0, s1, s2, op0, op1=None):
        eng = _e(eng)
        if op1 is None:
            S.op(eng, lambda h: h.tensor_scalar(out=out, in0=in0, scalar1=s1, scalar2=None, op0=op0),
                 reads=aps(in0, s1), writes=[out])
        else:
            S.op(eng, lambda h: h.tensor_scalar(out=out, in0=in0, scalar1=s1, scalar2=s2, op0=op0, op1=op1),
                 reads=aps(in0, s1, s2), writes=[out])

    def STT(out, in0, scalar, in1, op0, op1):
        S.op("dve", lambda h: h.scalar_tensor_tensor(out=out, in0=in0, scalar=scalar, in1=in1, op0=op0, op1=op1),
             reads=aps(in0, scalar, in1), writes=[out])

    def COPY(eng, out, in_):
        eng = _e(eng)
        if eng == "act":
            S.op("act", lambda h: h.copy(out=out, in_=in_), reads=[in_], writes=[out])
        else:
            S.op(eng, lambda h: h.tensor_copy(out=out, in_=in_), reads=[in_], writes=[out])

    def MEMSET(eng, out, val):
        eng = _e(eng)
        S.op(eng, lambda h: h.memset(out, val), writes=[out])

    def MM(out, lhsT, rhs, start, stop):
        S.op("pe", lambda h: h.matmul(out, lhsT=lhsT, rhs=rhs, start=start, stop=stop), reads=[lhsT, rhs], writes=[out])

    def TR(out, in_):
        S.op("pe", lambda h: h.transpose(out, in_, ident[:]), reads=[in_, ident[:]], writes=[out])

    def DMA(eng, out, in_, slot, reads=(), writes=()):
        S.op(eng, lambda h: h.dma_start(out=out, in_=in_), reads=list(reads), writes=list(writes), dma_slot=slot)

    def dbg(name, ap):
        if not debug:
            return
        o = nc.dram_tensor("dbg_" + name, list(ap.shape), ap.dtype, kind="ExternalOutput").ap()
        dbg_outs[name] = o
        DMA("pool", o, ap, "dbg_" + name, reads=[ap])


    def ps_bank():
        b = st_["ps"] % 6
        st_["ps"] += 1
        return PS[:, b, :]

    def ps_pair():
        if st_["ps"] % 2:
            st_["ps"] += 1
        b = st_["ps"] % 6
        st_["ps"] += 2
        return PS[:, b:b + 2, :].rearrange("p a n -> p (a n)")

    def tr_bank():
        b = st_["tr"] % 2
        st_["tr"] += 1
        return PTR[:, b, :]

    def wk():
        b = st_["wk"] % NWORK
        st_["wk"] += 1
        return work[:, b, :]

    def wload(name, blk):
        slot = st_["ring"] % NW
        st_["ring"] += 1
        dst = ring[:, slot, :, :]
        if st_["gi0"]:
            DMA("pool", dst, src_ap[(name, blk)], "cring%d" % slot, writes=[dst])
            DMA("sp", scr[name][blk], dst, "rst%d" % slot, reads=[dst], writes=[("scr_" + name, blk, blk + 1)])
        else:
            DMA("sp", dst, scr[name][blk], "ring%d" % slot, reads=[("scr_" + name, blk, blk + 1)], writes=[dst])
        return dst

    def cload(dst, src, i):
        DMA("pool", dst, src, "c%d" % i, writes=[dst])

    def vcol(name, i):
        o = VO[name] + i
        return vecs[:, o:o + 1]

    def x_load(gi):
        xr = xres[gi % 2]
        for sub in range(4):
            r0 = gi * T + sub * 128
            DMA("pool", xr[:, sub, :], x_d[r0:r0 + 128, :], "x%d_%d" % (gi % 2, sub), writes=[xr[:, sub, :]])

    def pos_load(gi):
        si, ti = divmod(gi, ntile)
        DMA("pool", posi[:], pos_d[si:si + 1, ti * T:(ti + 1) * T].partition_broadcast(128), "pos", writes=[posi[:]])

    cload(vecs[:], vecs_d, 0)
    pos_load(0)
    x_load(0)
    cload(ident[:], ident_d, 1)
    cload(wb1[:], n1_d.partition_broadcast(128), 4)
    cload(Dp[:], dp_d, 2)
    cload(QD[:], qd_d, 3)
    cload(wb2[:], n2_d.partition_broadcast(128), 5)
    cload(wfb[:], nf_d.partition_broadcast(128), 6)
    lam = vecs[:, VO["lam"]:VO["lam"] + 8]
    ACT(ctmp[:], lam, AF.Exp, scale=-1.0)
    ACT(ctmp[:], ctmp[:], AF.Ln, bias=1.0)
    TS("dve", c1h[:], ctmp[:], -4.0, None, ALU.mult)
    TS("dve", hbias[:, 0:8], vecs[:, VO["lbr"]:VO["lbr"] + 8], 0.5, None, ALU.mult)
    TS("dve", hbias[:, 8:16], vecs[:, VO["lbi"]:VO["lbi"] + 8], 0.5, None, ALU.mult)
    TS("dve", hbias[:, 16:32], vecs[:, VO["mgb"]:VO["mgb"] + 16], 0.5, None, ALU.mult)

    def cast_blk(name, blk, ap):
        src_ap[(name, blk)] = ap

    for b in range(14):
        cast_blk("w_in", b, w_in_d[:, b * 512:(b + 1) * 512].rearrange("(kc p) n -> p kc n", p=128))
    DMA("pool", Wr[:], wr_d.rearrange("n (kk p) j -> p n kk j", p=128), "c7", writes=[Wr[:]])
    DMA("pool", Wi[:], wi_d.rearrange("n (kk p) j -> p n kk j", p=128), "c8", writes=[Wi[:]])
    for b in range(2):
        cast_blk("w_ro", b, w_ro_d[:, b * 512:(b + 1) * 512].rearrange("(kc p) n -> p kc n", p=128))
        cast_blk("w_lo", b, w_lo_d[:, b * 512:(b + 1) * 512].rearrange("(kc p) n -> p kc n", p=128))
    for b in range(2):
        cast_blk("w_o", b, w_o_d[:, b * 512:(b + 1) * 512].rearrange("(kc p) n -> p kc n", p=128))
    for g in range(6):
        cast_blk("w_up", g, w_up_d[:, g * 512:(g + 1) * 512].rearrange("(kc p) n -> p kc n", p=128))
        cast_blk("w_up", 6 + g, w_up_d[:, DFF + g * 512:DFF + (g + 1) * 512].rearrange("(kc p) n -> p kc n", p=128))
    for half in range(2):
        for kg in range(3):
            cast_blk("w_dn", half * 3 + kg,
                     w_dn_d[kg * 1024:(kg + 1) * 1024, half * 512:(half + 1) * 512].rearrange("(kc p) n -> p kc n", p=128))

    def norm_stats(xr):
        sqrt_warm()
        for sub in range(4):
            ACT(xn[:, sub % 2, :], xr[:, sub, :], AF.Square, accum=stat[:, sub:sub + 1])
        ACT(stat[:, 4:8], stat[:, 0:4], AF.Sqrt, scale=1.0 / D, bias=RMS_EPS)
        S.op("dve", lambda h: h.reciprocal(out=stat[:, 8:12], in_=stat[:, 4:8]), reads=[stat[:, 4:8]], writes=[stat[:, 8:12]])

    def norm_scale(xr, wb):
        def mk(sub):
            def f():
                xs = xn[:, sub % 2, :]
                STT(xs, xr[:, sub, :], stat[:, 8 + sub:9 + sub], wb[:], ALU.mult, ALU.mult)
                tb = tr_bank()
                for c in range(8):
                    TR(tb[:, c * 128:(c + 1) * 128], xs[:, c * 128:(c + 1) * 128])
                COPY("act", hT[:, :, sub * 128:(sub + 1) * 128], tb.rearrange("p (c n) -> p c n", n=128))
            return f
        return [mk(sub) for sub in range(4)]

    def sqrt_warm():
        ACT(stat[:, 23:24], vcol("sgn", 0), AF.Sqrt, scale=0.0, bias=1.0)

    def tables(gi):
        ang = wk()
        ki = wk().bitcast(I32)
        TS("dve", ang, posi[:], vcol("invf", 0), None, ALU.mult)
        TS("dve", ki, ang, 1.0 / TWO_PI, None, ALU.mult)
        r1 = wk()
        STT(r1, ki, -CW1, ang, ALU.mult, ALU.add)
        STT(r1, ki, -CW2, r1, ALU.mult, ALU.add)
        m = wk()
        TS("dve", m, r1, float(np.pi), -TWO_PI, ALU.is_gt, ALU.mult)
        TT("dve", r1, r1, m, ALU.add)
        TS("dve", m, r1, -float(np.pi), TWO_PI, ALU.is_lt, ALU.mult)
        TT("dve", r1, r1, m, ALU.add)
        ACT(sin_t[:], r1, AF.Sin, scale=vcol("sgn", 0))
        cc = wk()
        TS("dve", cc, r1, float(np.pi / 2), None, ALU.add)
        TS("dve", m, cc, float(np.pi), -TWO_PI, ALU.is_gt, ALU.mult)
        TT("dve", cc, cc, m, ALU.add)
        ACT(cos_t[:], cc, AF.Sin)
        if gi + 1 < nseq * ntile:
            pos_load(gi + 1)

    def pre_phase(gi, spread=False):
        tables(gi)
        norm_stats(xres[gi % 2])
        subs = norm_scale(xres[gi % 2], wb1)
        if not spread:
            for f in subs:
                f()
            return []
        subs[0]()
        return subs[1:]

    def proj_fm(wblk, j):
        ps = ps_bank()
        for kc in range(8):
            MM(ps, wblk[:, kc, j * 128:(j + 1) * 128], hT[:, kc, :], kc == 0, kc == 7)
        return ps

    def rotary(ps, dst, qdh=None):
        qs = wk()
        COPY("act", qs, ps)
        t1 = wk()
        t2 = wk()
        TT("dve", t1, qs, cos_t[:], ALU.mult)
        TT("dve", t2[0:64, :], qs[64:128, :], sin_t[64:128, :], ALU.mult)
        TT("dve", t2[64:128, :], qs[0:64, :], sin_t[0:64, :], ALU.mult)
        TT("dve", dst, t1, t2, ALU.add)

    lcw = lambda k, c: vcol("lcw", k * 8 + c)
    fcw = lambda k, c: vcol("fcw", k * NCH_FF + c)

    def main_phase(gi):
        si, ti = divmod(gi, ntile)
        first = (ti == 0)
        xr_all = xres[gi % 2]
        if gi + 1 < nseq * ntile:
            x_load(gi + 1)
        if first:
            MEMSET("pool", Sf[:], 0.0)
            MEMSET("pool", Sbf[:, :, 0, :], 0.0)
            MEMSET("pool", halo_l[:], 0.0)
            MEMSET("pool", halo_f[:], 0.0)
        if debug and gi == 0:
            dbg("cos", cos_t[:])
            dbg("sin", sin_t[:])
            dbg("h1T", hT[:])
        def v_half(half):
            wv = wload("w_in", 2 + half)
            for sub in range(4):
                ps = ps_bank()
                for kc in range(8):
                    MM(ps, hT[:, kc, sub * 128:(sub + 1) * 128], wv[:, kc, :], kc == 0, kc == 7)
                COPY("act", v_tm[:, sub, half * 512:(half + 1) * 512], ps)

        wq = wload("w_in", 0)
        for h in range(HEADS):
            ps = proj_fm(wq, h)
            rotary(ps, qd[:, h, :], QD[:, h, :])
        v_half(0)
        wk_ = wload("w_in", 1)
        for h in range(HEADS):
            ps = proj_fm(wk_, h)
            rotary(ps, krot[:, h, :])
        v_half(1)
        def xlru_half(half):
            wx = wload("w_in", 6 + half)
            todo = []
            for j in range(4):
                c = half * 4 + j
                ps = proj_fm(wx, j)
                xs = rawb[:, c % 4, :]
                COPY("act", xs[:, 3:T + 3], ps)
                COPY("pool", xs[:, 0:3], halo_l[:, c, :])
                todo.append((c, xs))

            def conv():
                for c, xs in todo:
                    if half == 1:
                        acc = wk()
                        tmp = wk()
                        TS("pool", acc, xs[:, 3:T + 3], lcw(3, c), vcol("lcb", c), ALU.mult, ALU.add)
                        for k in (2, 1, 0):
                            TS("pool", tmp, xs[:, k:T + k], lcw(k, c), 0.0, ALU.mult, ALU.add)
                            TT("pool", xcb[:, c, :] if k == 0 else acc, acc, tmp, ALU.add)
                        COPY("pool", halo_l[:, c, :], xs[:, T:T + 3])
                        continue
                    acc = wk()
                    TS("dve", acc, xs[:, 3:T + 3], lcw(3, c), vcol("lcb", c), ALU.mult, ALU.add)
                    STT(acc, xs[:, 2:T + 2], lcw(2, c), acc, ALU.mult, ALU.add)
                    STT(acc, xs[:, 1:T + 1], lcw(1, c), acc, ALU.mult, ALU.add)
                    STT(xcb[:, c, :], xs[:, 0:T], lcw(0, c), acc, ALU.mult, ALU.add)
                    COPY("pool", halo_l[:, c, :], xs[:, T:T + 3])
            return conv

        for hp in range(2):
            tb = tr_bank()
            for hh in range(2):
                h = hp * 2 + hh
                for b in range(NB):
                    TR(tb[:, (hh * NB + b) * 128:(hh * NB + b + 1) * 128], krot[:, h, b * 128:(b + 1) * 128])
            for hh in range(2):
                h = hp * 2 + hh
                ACT(kd[:, h, :, :].rearrange("p b d -> p (b d)"), tb[:, hh * 512:(hh + 1) * 512], AF.Copy,
                    scale=vcol("kdec", h))
        for h in range(HEADS):
            ps = ps_bank()
            for b in range(NB):
                MM(ps[:, b * 128:(b + 1) * 128], krot[:, h, b * 128:(b + 1) * 128], qd[:, h, b * 128:(b + 1) * 128], True, True)
            TT("dve", PTb[:, h, :].rearrange("p (b i) -> p b i", i=128), ps.rearrange("p (b i) -> p b i", i=128),
               Dp[:, h, :].unsqueeze(1).to_broadcast([128, NB, 128]), ALU.mult)
        xlru_half(0)()
        conv1 = xlru_half(1)

        def gret_half(half):
            wg = wload("w_in", 4 + half)
            for j in range(4):
                c = half * 4 + j
                ps = proj_fm(wg, j)
                ACT(sg[:, c, :], ps, AF.Silu)
                TS("dve", sg[:, c, :], sg[:, c, :], vcol("gnw", c), None, ALU.mult)

        upairs = []
        for h in range(HEADS):
            pp = ps_pair()
            for b in range(NB):
                MM(pp[:, b * DV:(b + 1) * DV], kd[:, h, b, :], v_tm[:, b, h * DV:(h + 1) * DV], True, True)
            upairs.append(pp)
            if h % 2 == 1:
                for b in range(NB):
                    for h2 in (h - 1, h):
                        u_ = upairs[h2][:, b * DV:(b + 1) * DV]
                        STT(Sbf[:, h2, b + 1, :], Sf[:, h2, :], cdec[h2], u_, ALU.mult, ALU.add)
                        STT(Sf[:, h2, :], Sf[:, h2, :], cdec[h2], u_, ALU.mult, ALU.add)
                gret_half(h // 2)
        if debug and gi == 0:
            dbg("qd", qd[:])
            dbg("krot", krot[:])
            dbg("v", v_tm)
            dbg("xcb", xcb)

        wy = [None, None]

        def o_mm(b):
            pp = ps_pair()
            for h in range(HEADS):
                o_ = pp[:, h * DV:(h + 1) * DV]
                MM(o_, PTb[:, h, b * 128:(b + 1) * 128], v_tm[:, b, h * DV:(h + 1) * DV], True, False)
                MM(o_, qd[:, h, b * 128:(b + 1) * 128], Sbf[:, h, b, :], False, True)
            return pp

        def o_norm(b, pp):
            for h in range(HEADS):
                S.op("dve", lambda hd, h=h, pp=pp: hd.bn_stats(out=st6[:, h, :], in_=pp[:, h * DV:(h + 1) * DV]),
                     reads=[pp[:, h * DV:(h + 1) * DV]], writes=[st6[:, h, :]])
            for h in range(HEADS):
                S.op("dve", lambda hd, h=h: hd.bn_aggr(out=mv[:, h, :], in_=st6[:, h, :]),
                     reads=[st6[:, h, :]], writes=[mv[:, h, :]])
            TT("dve", gstat[:, 0, :], mv[:, :, 1], vecs[:, VO["epsq"]:VO["epsq"] + HEADS], ALU.add)
            ACT(gstat[:, 0, :], gstat[:, 0, :], AF.Sqrt)
            S.op("dve", lambda hd: hd.reciprocal(out=gstat[:, 1, :], in_=gstat[:, 0, :]), reads=[gstat[:, 0, :]], writes=[gstat[:, 1, :]])
            STT(gstat[:, 2, :], mv[:, :, 0], -1.0, gstat[:, 1, :], ALU.mult, ALU.mult)
            ob = xn[:, b % 2, :]
            for h in range(HEADS):
                ACT(ob[:, h * DV:(h + 1) * DV], pp[:, h * DV:(h + 1) * DV], AF.Identity,
                    bias=gstat[:, 2, h:h + 1], scale=gstat[:, 1, h:h + 1])
            if debug and gi == 0 and b == 0:
                dbg("onb0", ob)

        def o_tr(b):
            ob = xn[:, b % 2, :]
            tb = tr_bank()
            for c in range(8):
                TR(tb[:, c * 128:(c + 1) * 128], ob[:, c * 128:(c + 1) * 128])
            og_b = sg[:, :, b * 128:(b + 1) * 128]
            TT("dve", og_b, tb.rearrange("p (c n) -> p c n", n=128), og_b, ALU.mult)

        def ylru(half):
            wyb = wload("w_in", 8 + half)
            for j in range(4):
                c = half * 4 + j
                ps = proj_fm(wyb, j)
                ACT(gy[:, c, :], ps, AF.Gelu_apprx_tanh)

        pp0 = o_mm(0)
        o_norm(0, pp0)
        pp1 = o_mm(1)
        o_norm(1, pp1)
        conv1()
        ylru(0)
        o_tr(0)
        pp2 = o_mm(2)
        o_norm(2, pp2)
        o_tr(1)
        pp3 = o_mm(3)
        o_norm(3, pp3)
        ylru(1)
        o_tr(2)
        o_tr(3)
        if not (ti == ntile - 1):
            COPY("pool", Sbf[:, :, 0, :], Sbf[:, :, NB, :])

        mg_list = [(which, half) for which in range(2) for half in range(2)]

        def merge_gate_block(which, half):
            dst = gr if which == 0 else gl
            wg = wload("w_in", 10 + which * 2 + half)
            for j in range(4):
                c = half * 4 + j
                ps = proj_fm(wg, j)
                ACT(dst[:, c, :], ps, AF.Tanh, bias=hbias[:, 16 + which * 8 + c:17 + which * 8 + c], scale=0.5)
                TS("pool", dst[:, c, :], dst[:, c, :], 0.5, 0.5, ALU.mult, ALU.add)

        for cp in range(4):
            if cp < 3:
                merge_gate_block(*mg_list[cp])
            bufs = []
            for c in (2 * cp, 2 * cp + 1):
                n = c // 2
                cc_ = c % 2
                psr = ps_bank()
                for kk in range(2):
                    MM(psr, Wr[:, n, kk, cc_ * 128:(cc_ + 1) * 128], xcb[:, 2 * n + kk, :], kk == 0, kk == 1)
                psi = ps_bank()
                for kk in range(2):
                    MM(psi, Wi[:, n, kk, cc_ * 128:(cc_ + 1) * 128], xcb[:, 2 * n + kk, :], kk == 0, kk == 1)
                r_ = wk()
                i_ = wk()
                a_ = wk()
                ACT(r_, psr, AF.Tanh, bias=hbias[:, c:c + 1], scale=0.5)
                ACT(i_, psi, AF.Tanh, bias=hbias[:, 8 + c:9 + c], scale=0.5)
                ACT(a_, r_, AF.Exp, bias=c1h[:, c:c + 1], scale=c1h[:, c:c + 1])
                TT("dve", r_, a_, a_, ALU.mult)
                bufs.append((c, r_, i_, a_))
            for (c, r_, i_, a_) in bufs:
                ACT(r_, r_, AF.Sqrt, scale=-1.0, bias=1.0)
            for (c, r_, i_, a_) in bufs:
                STT(i_, i_, 1.0, r_, ALU.add, ALU.mult)
                STT(i_, i_, 0.5, xcb[:, c, :], ALU.mult, ALU.mult)
                hl = wk()
                init = 0.0 if first else hstate[:, c:c + 1]
                S.op("dve", lambda hd, hl=hl, a_=a_, i_=i_, init=init: hd.tensor_tensor_scan(
                    out=hl, data0=a_, data1=i_, initial=init, op0=ALU.mult, op1=ALU.add),
                    reads=aps(a_, i_, init), writes=[hl])
                COPY("pool", hstate[:, c:c + 1], hl[:, T - 1:T])
                if debug and gi == 0 and c == 0:
                    dbg("hl0", hl)
                TT("dve", gy[:, c, :], hl, gy[:, c, :], ALU.mult)
        merge_gate_block(*mg_list[3])
        sqrt_warm()
        if debug and gi == 0:
            dbg("og", sg)
            dbg("pg", gy)
        for half in range(2):
            wro = wload("w_ro", half)
            wlo = wload("w_lo", half)
            m1s = []
            for j in range(4):
                oc = half * 4 + j
                psa = ps_bank()
                for kc in range(8):
                    MM(psa, wro[:, kc, j * 128:(j + 1) * 128], sg[:, kc, :], kc == 0, kc == 7)
                m1 = wk()
                TT("dve", m1, psa, gr[:, oc, :], ALU.mult)
                m1s.append(m1)
            for j in range(4):
                oc = half * 4 + j
                psb = ps_bank()
                for kc in range(8):
                    MM(psb, wlo[:, kc, j * 128:(j + 1) * 128], gy[:, kc, :], kc == 0, kc == 7)
                m2 = wk()
                TT("dve", m2, psb, gl[:, oc, :], ALU.mult)
                TT("pool", mixT[:, oc, :], m1s[j], m2, ALU.add)
        if debug and gi == 0:
            dbg("mixT", mixT)
        wo = [wload("w_o", 0), wload("w_o", 1)]

        def n2_slot(sub):
            if sub < 2:
                return xn[:, sub, :]
            return gl[:, 4 + (sub - 2) * 2:6 + (sub - 2) * 2, :].rearrange("p a n -> p (a n)")

        def n2_tr(sub):
            xs = n2_slot(sub)
            tb = tr_bank()
            for c in range(8):
                TR(tb[:, c * 128:(c + 1) * 128], xs[:, c * 128:(c + 1) * 128])
            COPY("act", hT[:, :, sub * 128:(sub + 1) * 128], tb.rearrange("p (c n) -> p c n", n=128))

        def n2_stt(sub):
            STT(n2_slot(sub), xr_all[:, sub, :], stat[:, 8 + sub:9 + sub], wb2[:], ALU.mult, ALU.mult)

        for sub in range(4):
            for half in range(2):
                ps = ps_bank()
                for kc in range(8):
                    MM(ps, mixT[:, kc, sub * 128:(sub + 1) * 128], wo[half][:, kc, :], kc == 0, kc == 7)
                xr = xr_all[:, sub, half * 512:(half + 1) * 512]
                TT("dve", xr, xr, ps, ALU.add)
            if sub < 3:
                ACT(gl[:, (sub % 2) * 2:(sub % 2) * 2 + 2, :].rearrange("p a n -> p (a n)"), xr_all[:, sub, :], AF.Square,
                    accum=stat[:, sub:sub + 1])
            if sub == 2:
                ACT(stat[:, 4:7], stat[:, 0:3], AF.Sqrt, scale=1.0 / D, bias=RMS_EPS)
                S.op("dve", lambda h: h.reciprocal(out=stat[:, 8:11], in_=stat[:, 4:7]), reads=[stat[:, 4:7]], writes=[stat[:, 8:11]])
                n2_stt(0)
                n2_stt(1)
        if debug and gi == 0:
            dbg("x1", xr_all[:])
        n2_stt(2)
        ACT(gl[:, 2:4, :].rearrange("p a n -> p (a n)"), xr_all[:, 3, :], AF.Square, accum=stat[:, 3:4])
        ACT(stat[:, 7:8], stat[:, 3:4], AF.Sqrt, scale=1.0 / D, bias=RMS_EPS)
        S.op("dve", lambda h: h.reciprocal(out=stat[:, 11:12], in_=stat[:, 7:8]), reads=[stat[:, 7:8]], writes=[stat[:, 11:12]])
        n2_stt(3)
        n2_tr(0)
        n2_tr(1)
        n2_tr(2)
        n2_tr(3)
        for g in range(6):
            wgt = wload("w_up", g)
            wvl = wload("w_up", 6 + g)
            for j in range(4):
                oc = g * 4 + j
                psg = proj_fm(wgt, j)
                psv = proj_fm(wvl, j)
                gs = rawb[:, oc % 4, :]
                COPY("act", gs[:, 2:T + 2], psg)
                COPY("pool", gs[:, 0:2], halo_f[:, oc, :])
                acc = wk()
                TS("dve", acc, gs[:, 2:T + 2], fcw(2, oc), vcol("fcb", oc), ALU.mult, ALU.add)
                STT(acc, gs[:, 1:T + 1], fcw(1, oc), acc, ALU.mult, ALU.add)
                STT(acc, gs[:, 0:T], fcw(0, oc), acc, ALU.mult, ALU.add)
                COPY("pool", halo_f[:, oc, :], gs[:, T:T + 2])
                ACT(acc, acc, AF.Gelu_apprx_tanh)
                TT("dve", pf[:, oc, :], acc, psv, ALU.mult)
        if debug and gi == 0:
            dbg("pf", pf)

    def down_phase(gi, mid_hook=None):
        xr_all = xres[gi % 2]
        pending = []
        for half in range(2):
            pss = [ps_bank() for _ in range(4)]
            for kg in range(3):
                wd = wload("w_dn", half * 3 + kg)
                for pair in ((0, 1), (2, 3)):
                    for kc in range(8):
                        c = kg * 8 + kc
                        for sub in pair:
                            MM(pss[sub], pf[:, c, sub * 128:(sub + 1) * 128], wd[:, kc, :], c == 0, c == NCH_FF - 1)
                    if half == 1 and pending:
                        pending.pop(0)()
                if half == 0 and kg == 2 and mid_hook is not None:
                    pending = mid_hook()
            for sub in range(4):
                xr = xr_all[:, sub, half * 512:(half + 1) * 512]
                TT("dve", xr, xr, pss[sub], ALU.add)
        sqrt_warm()
        for sub in range(4):
            ACT(xn[:, sub % 2, :], xr_all[:, sub, :], AF.Square, accum=stat[:, 12 + sub:13 + sub])
        ACT(stat[:, 16:20], stat[:, 12:16], AF.Sqrt, scale=1.0 / D, bias=RMS_EPS)
        S.op("dve", lambda h: h.reciprocal(out=stat[:, 20:24], in_=stat[:, 16:20]), reads=[stat[:, 16:20]], writes=[stat[:, 20:24]])
        for sub in range(4):
            STT(xr_all[:, sub, :], xr_all[:, sub, :], stat[:, 20 + sub:21 + sub], wfb[:], ALU.mult, ALU.mult)
            r0 = gi * T + sub * 128
            DMA("pool", out_d[r0:r0 + 128, :], xr_all[:, sub, :], "o%d_%d" % (gi % 2, sub), reads=[xr_all[:, sub, :]])

    ntot = nseq * ntile
    pre_phase(0)
    for gi in range(ntot):
        st_["gi0"] = (gi == 0)
        main_phase(gi)
        if gi + 1 < ntot:
            down_phase(gi, mid_hook=lambda gi=gi: pre_phase(gi + 1, spread=True))
        else:
            down_phase(gi)

    counts = S.emit()
    st.close()
    return nc, counts, dbg_outs


def prep_inputs(inputs, core, nseq=4, ntile=4):
    f = lambda a: np.ascontiguousarray(np.asarray(a), dtype=np.float32)
    dpc, qdc, kdecc, cdec, invfc, sgnc, epsqc = host_consts()
    x = f(inputs["x"])
    ntok = nseq * ntile * T
    xs = x[core * nseq:(core + 1) * nseq, :ntile * T, :].reshape(ntok, D)
    pos = np.ascontiguousarray(np.asarray(inputs["positions"])[core * nseq:(core + 1) * nseq, :ntile * T].astype(np.int32))
    fm = lambda v, nch: f(v).reshape(nch, 128).T
    cols = {
        "mgb": np.concatenate([fm(inputs["merge_gate_b"][0, 0], 8), fm(inputs["merge_gate_b"][0, 1], 8)], axis=1),
        "gnw": fm(inputs["ret_gn_w"][0], 8),
        "lcw": np.concatenate([fm(inputs["lru_conv_w"][0, k], 8) for k in range(4)], axis=1),
        "lcb": fm(inputs["lru_conv_b"][0], 8),
        "lbr": fm(np.asarray(inputs["lru_b_r"][0]).reshape(-1), 8),
        "lbi": fm(np.asarray(inputs["lru_b_i"][0]).reshape(-1), 8),
        "lam": fm(inputs["lru_lambda"][0], 8),
        "fcw": np.concatenate([fm(inputs["ffn_conv_w"][0, k], NCH_FF) for k in range(3)], axis=1),
        "fcb": fm(inputs["ffn_conv_b"][0], NCH_FF),
        "kdec": kdecc,
        "invf": invfc[:, None],
        "sgn": sgnc[:, None],
        "epsq": epsqc,
    }
    vecs = np.zeros((128, NV), np.float32)
    for k, o in VO.items():
        a = cols[k]
        vecs[:, o:o + a.shape[1]] = a
    return {
        "x": np.ascontiguousarray(xs),
        "pos": pos,
        "w_in": f(inputs["w_in"][0]),
        "w_ret_o": f(inputs["w_ret_o"][0]),
        "w_lru_o": f(inputs["w_lru_o"][0]),
        "w_out": f(inputs["w_out"][0]),
        "ffn_w_up": f(inputs["ffn_w_up"][0]),
        "ffn_w_down": f(inputs["ffn_w_down"][0]),
        "lru_w_r": f(inputs["lru_w_r"][0]),
        "lru_w_i": f(inputs["lru_w_i"][0]),
        "norm1_w": f(inputs["norm1_w"][0])[None, :],
        "norm2_w": f(inputs["norm2_w"][0])[None, :],
        "norm_f_w": f(inputs["norm_f_w"]).reshape(1, D),
        "vecs": vecs,
        "dpc": dpc,
        "qdc": qdc,
        "identc": np.eye(128, dtype=np.float32),
    }


_CACHE = {}


def kernel(**inputs):
    ncores = 8
    if "nc" not in _CACHE:
        _CACHE["nc"] = build(4, 4)[0]
    nc = _CACHE["nc"]
    in_maps = [prep_inputs(inputs, c) for c in range(ncores)]
    res = run_bass_kernel_spmd(nc, in_maps, core_ids=list(range(ncores)))
    outs = [np.asarray(r["out"], dtype=np.float32).reshape(4, SEQ, D) for r in res.results]
    return np.concatenate(outs, axis=0)
```
